# Optimizing a Trainium2 kernel written in Bass

```python
import math
import jax, jax.numpy as jnp
from jax import lax
import numpy as np

D_MODEL = 1024
BATCH = 16
SEQ = 4096
DEPTH = 4
DEC_BATCH = 8
DEC_SEQ = 4096
PAST_LEN = 128

HEAD_DIM = 64
A_HEADS = 8
A_KV_HEADS = 2
B_HEADS = 8
MIX_WIDTH = (A_HEADS + B_HEADS) * HEAD_DIM
GRID_W = 64
ROPE_THETA = 10000.0
Q_BLOCK = 128
DILATED_BRANCHES = ((128, 1), (512, 4), (2048, 16))
N_BUCKETS = 32
REL_MAX_DIST = 1024
D_FF = 2816
CONV_WIDTH = 3
EPS = 1e-6
NEG_INF = -1e30
ATTN_SCALE = HEAD_DIM ** -0.5
SPLIT_SIZES = (A_HEADS * HEAD_DIM, A_KV_HEADS * HEAD_DIM, A_KV_HEADS * HEAD_DIM,
               B_HEADS * HEAD_DIM, B_HEADS * HEAD_DIM, B_HEADS * HEAD_DIM)
IN_WIDTH = sum(SPLIT_SIZES)
SPLIT_POINTS = tuple(int(v) for v in np.cumsum(SPLIT_SIZES)[:-1])

kernel_name = "hybrid_gqa_dilated_convffn_encoder"


def _rmsnorm(x, g):
    xf = x.astype(jnp.float32)
    y = xf * lax.rsqrt(jnp.mean(xf * xf, axis=-1, keepdims=True) + EPS)
    return (y * g.astype(jnp.float32)).astype(x.dtype)


def _axial_rope(S, dtype):
    rows = S // GRID_W
    row = jnp.repeat(jnp.arange(rows), GRID_W).astype(jnp.float32)
    col = jnp.tile(jnp.arange(GRID_W), rows).astype(jnp.float32)
    n = HEAD_DIM // 4
    inv = ROPE_THETA ** (-jnp.arange(n, dtype=jnp.float32) / n)
    ang = jnp.concatenate([row[:, None] * inv, col[:, None] * inv], axis=-1)
    return jnp.cos(ang).astype(dtype), jnp.sin(ang).astype(dtype)


def _apply_rope(x, cos, sin):
    xr = x.reshape(x.shape[:-1] + (HEAD_DIM // 2, 2))
    x0, x1 = xr[..., 0], xr[..., 1]
    c = cos[None, :, None, :]
    s = sin[None, :, None, :]
    return jnp.stack([x0 * c - x1 * s, x0 * s + x1 * c], axis=-1).reshape(x.shape)


def _mixer_a(q, k, v):
    B, S = q.shape[:2]
    nq = S // Q_BLOCK
    grp = A_HEADS // A_KV_HEADS
    qb = q.reshape(B, nq, Q_BLOCK, A_KV_HEADS, grp, HEAD_DIM).transpose(1, 0, 2, 3, 4, 5)

    def block(qi):
        s = jnp.einsum('bqkgd,bskd->bkgqs', qi, k, preferred_element_type=jnp.float32) * ATTN_SCALE
        p = jax.nn.softmax(s, axis=-1).astype(v.dtype)
        return jnp.einsum('bkgqs,bskd->bqkgd', p, v)

    o = lax.map(block, qb)
    return o.transpose(1, 0, 2, 3, 4, 5).reshape(B, S, A_HEADS * HEAD_DIM)


def _t5_bucket(rel):
    nb = N_BUCKETS // 2
    max_exact = nb // 2
    ret = jnp.where(rel > 0, nb, 0)
    n = jnp.abs(rel)
    large = max_exact + (jnp.log(jnp.maximum(n, 1).astype(jnp.float32) / max_exact)
                         / math.log(REL_MAX_DIST / max_exact) * (nb - max_exact)).astype(jnp.int32)
    large = jnp.minimum(large, nb - 1)
    return ret + jnp.where(n < max_exact, n, large)


def _dilated_branch(q, k, v, rel_bias, window, dilation):
    B, S, H, D = q.shape
    half = window // (2 * dilation)
    C = half
    L = S // dilation
    nC = -(-L // C)
    Lp = nC * C

    def sub(x):
        return x.reshape(B, L, dilation, H, D).transpose(0, 2, 1, 3, 4)

    qs = jnp.pad(sub(q), ((0, 0), (0, 0), (0, Lp - L), (0, 0), (0, 0))).reshape(B, dilation, nC, C, H, D)

    def windows(x):
        xp = jnp.pad(sub(x), ((0, 0), (0, 0), (C, C + Lp - L), (0, 0), (0, 0)))
        xp = xp.reshape(B, dilation, nC + 2, C, H, D)
        return jnp.concatenate([xp[:, :, :-2], xp[:, :, 1:-1], xp[:, :, 2:]], axis=3)

    kw = windows(k)
    vw = windows(v)
    i = jnp.arange(C)[:, None]
    j = jnp.arange(3 * C)[None, :]
    delta = j - C - i
    key_pos = jnp.arange(nC)[:, None] * C + jnp.arange(3 * C)[None, :] - C
    valid = (key_pos >= 0) & (key_pos < L)
    mask = (jnp.abs(delta) <= half)[None] & valid[:, None, :]
    bias = rel_bias[_t5_bucket(delta * dilation)].astype(jnp.float32).transpose(2, 0, 1)
    s = jnp.einsum('brcqhd,brckhd->brchqk', qs, kw, preferred_element_type=jnp.float32) * ATTN_SCALE + bias
    s = jnp.where(mask[None, None, :, None], s, NEG_INF)
    m = jnp.max(s, axis=-1, keepdims=True)
    e = jnp.exp(s - m)
    den = jnp.sum(e, axis=-1, keepdims=True)
    o = jnp.einsum('brchqk,brckhd->brcqhd', (e / den).astype(v.dtype), vw)
    lse = (m + jnp.log(den))[..., 0]
    o = o.reshape(B, dilation, Lp, H, D)[:, :, :L].transpose(0, 2, 1, 3, 4).reshape(B, S, H, D)
    lse = lse.transpose(0, 1, 2, 4, 3).reshape(B, dilation, Lp, H)[:, :, :L]
    lse = lse.transpose(0, 2, 1, 3).reshape(B, S, H)
    return o, lse


def _mixer_b(q, k, v, rel_bias):
    B, S = q.shape[:2]
    outs, lses = [], []
    for window, dilation in DILATED_BRANCHES:
        o, l = _dilated_branch(q, k, v, rel_bias, window, dilation)
        outs.append(o)
        lses.append(l)
    w = jax.nn.softmax(jnp.stack(lses, axis=0), axis=0)
    o = jnp.sum(w[..., None].astype(q.dtype) * jnp.stack(outs, axis=0), axis=0)
    return o.reshape(B, S, B_HEADS * HEAD_DIM)


def _dwconv_centred(h, w, b):
    S = h.shape[1]
    pad = CONV_WIDTH // 2
    hp = jnp.pad(h, ((0, 0), (pad, pad), (0, 0)))
    out = b
    for t in range(CONV_WIDTH):
        out = out + hp[:, t:t + S] * w[t]
    return out


def _trunk(x, attn_norm, w_in, q_norm, k_norm, rel_bias, w_out, ffn_norm, w_up, conv_w, conv_b, w_down, final_norm):
    B, S, _ = x.shape
    cos, sin = _axial_rope(S, x.dtype)
    for l in range(DEPTH):
        h = _rmsnorm(x, attn_norm[l])
        p = h @ w_in[l]
        qa, ka, va, qb, kb, vb = jnp.split(p, SPLIT_POINTS, axis=-1)
        qa = _apply_rope(_rmsnorm(qa.reshape(B, S, A_HEADS, HEAD_DIM), q_norm[l]), cos, sin)
        ka = _apply_rope(_rmsnorm(ka.reshape(B, S, A_KV_HEADS, HEAD_DIM), k_norm[l]), cos, sin)
        va = va.reshape(B, S, A_KV_HEADS, HEAD_DIM)
        oa = _mixer_a(qa, ka, va)
        ob = _mixer_b(qb.reshape(B, S, B_HEADS, HEAD_DIM), kb.reshape(B, S, B_HEADS, HEAD_DIM),
                      vb.reshape(B, S, B_HEADS, HEAD_DIM), rel_bias)
        x = x + jnp.concatenate([oa, ob], axis=-1) @ w_out[l]
        h2 = _rmsnorm(x, ffn_norm[l])
        u = _dwconv_centred(h2 @ w_up[l], conv_w[l], conv_b[l])
        g, val = jnp.split(u, 2, axis=-1)
        x = x + (jax.nn.silu(g) * val) @ w_down[l]
    return _rmsnorm(x, final_norm)


def setup_inputs(seed: int = 0) -> dict:
    key = jax.random.key(seed)
    ks = jax.random.split(key, 16)
    f32 = jnp.float32
    nrm = lambda k, shape, scale: jax.random.normal(k, shape, f32) * scale
    return {
        "x_prompt": nrm(ks[0], (BATCH, SEQ, D_MODEL), 1.0),
        "x_sample": nrm(ks[1], (DEC_BATCH, DEC_SEQ, D_MODEL), 1.0),
        "attn_norm": 1.0 + nrm(ks[2], (DEPTH, D_MODEL), 0.01),
        "w_in": nrm(ks[3], (DEPTH, D_MODEL, IN_WIDTH), D_MODEL ** -0.5),
        "q_norm": 1.0 + nrm(ks[4], (DEPTH, HEAD_DIM), 0.01),
        "k_norm": 1.0 + nrm(ks[5], (DEPTH, HEAD_DIM), 0.01),
        "rel_bias": nrm(ks[6], (N_BUCKETS, B_HEADS), 0.5),
        "w_out": nrm(ks[7], (DEPTH, MIX_WIDTH, D_MODEL), MIX_WIDTH ** -0.5),
        "ffn_norm": 1.0 + nrm(ks[8], (DEPTH, D_MODEL), 0.01),
        "w_up": nrm(ks[9], (DEPTH, D_MODEL, 2 * D_FF), D_MODEL ** -0.5),
        "conv_w": nrm(ks[10], (DEPTH, CONV_WIDTH, 2 * D_FF), CONV_WIDTH ** -0.5),
        "conv_b": nrm(ks[11], (DEPTH, 2 * D_FF), 0.01),
        "w_down": nrm(ks[12], (DEPTH, D_FF, D_MODEL), D_FF ** -0.5),
        "final_norm": 1.0 + nrm(ks[13], (D_MODEL,), 0.01),
    }


def reference(x_prompt, x_sample, attn_norm, w_in, q_norm, k_norm, rel_bias, w_out, ffn_norm, w_up, conv_w, conv_b, w_down, final_norm):
    y_prompt = _trunk(x_prompt, attn_norm, w_in, q_norm, k_norm, rel_bias, w_out, ffn_norm, w_up, conv_w, conv_b, w_down, final_norm)
    y_sample = _trunk(x_sample, attn_norm, w_in, q_norm, k_norm, rel_bias, w_out, ffn_norm, w_up, conv_w, conv_b, w_down, final_norm)
    return (y_prompt, y_sample)
```

```python
import math
from contextlib import ExitStack
import numpy as np
import concourse.bass as bass
import concourse.mybir as mybir
from concourse.bass_utils import run_bass_kernel_spmd

F32 = mybir.dt.float32
BF16 = mybir.dt.bfloat16
ALU = mybir.AluOpType
AF = mybir.ActivationFunctionType
AX = mybir.AxisListType

S = 4096
D = 1024
NLAYER = 4
INW = 2304
DFF = 2816
NPAIR = 22
EPS = 1e-6
PADV = 1024
PADK = 1024
NCORES = 8
SEQ_PER_CORE = 3
BRANCHES = ((1, 0, 33), (4, 33, 9), (16, 69, 3))
NVCH = 117
SEM_ROT = 30000
DBG_HEADS = None
DBG_LEVEL = 9
DBG_BSTAGE = 4
DBG_NBLK = None


class SemObj:
    def __init__(self):
        self.h = None
        self.cnt = 0


class Prog:
    def __init__(self):
        self.sems = []
        self.queues = []

    def newsem(self):
        s = SemObj()
        self.sems.append(s)
        return s


class Slot:
    def __init__(self, prog):
        self.prog = prog
        self.cur = prog.newsem()


class Q:
    def __init__(self, name, prog):
        self.name = name
        self.prog = prog
        self.ops = []
        self.cur = prog.newsem()
        self.seen = {}
        self.last = None
        prog.queues.append(self)

    def do(self, fn, sig=False):
        if sig:
            if self.cur.cnt >= SEM_ROT:
                self.cur = self.prog.newsem()
            self.cur.cnt += 1
            tk = (self.cur, self.cur.cnt)
            self.ops.append((0, fn, self.cur))
            self.last = tk
            return tk
        self.ops.append((0, fn, None))
        return None

    def wait(self, *tks):
        for tk in tks:
            if tk is None:
                continue
            so, v = tk
            if self.seen.get(so, 0) >= v:
                continue
            self.seen[so] = v
            self.ops.append((1, so, v))

    def dma(self, out, in_, slot):
        if slot.cur.cnt >= SEM_ROT:
            slot.cur = self.prog.newsem()
        slot.cur.cnt += 16
        self.ops.append((2, out, in_, slot.cur))
        return (slot.cur, slot.cur.cnt)

    def replay(self, e):
        for op in self.ops:
            if op[0] == 0:
                ins = op[1](e)
                if op[2] is not None:
                    ins.then_inc(op[2].h, 1)
            elif op[0] == 1:
                e.wait_ge(op[1].h, op[2])
            else:
                e.dma_start(out=op[1], in_=op[2]).then_inc(op[3].h, 16)


class Arena:
    def __init__(self, ap, nbytes):
        self.ap = ap
        self.nbytes = nbytes
        self.off = 0

    def reset(self):
        self.off = 0

    def _get(self, nbytes):
        nbytes = (nbytes + 63) // 64 * 64
        o = self.off
        self.off += nbytes
        assert self.off <= self.nbytes, ("arena overflow", self.off, self.nbytes)
        return self.ap[:, o // 2:(o + nbytes) // 2]

    def bf(self, n):
        return self._get(2 * n)[:, 0:n]

    def f32(self, n):
        return self._get(4 * n).bitcast(F32)[:, 0:n]


def t5_bucket_np(rel):
    nb = 16
    max_exact = 8
    ret = np.where(rel > 0, nb, 0)
    n = np.abs(rel)
    lg = (np.log(np.maximum(n, 1).astype(np.float32) / np.float32(max_exact))
          / np.float32(math.log(1024 / max_exact)) * np.float32(nb - max_exact))
    large = max_exact + lg.astype(np.int32)
    large = np.minimum(large, nb - 1)
    return ret + np.where(n < max_exact, n, large)


def make_masks():
    m = np.zeros((3, 32, 128, 256), np.float32)
    p = np.arange(128)[:, None]
    i = np.arange(128)[None, :]
    for br, (d, _, _) in enumerate(BRANCHES):
        for mm in range(2):
            delta = 128 * mm - 64 + p - i
            valid = np.abs(delta) <= 64
            bk = t5_bucket_np(delta * d)
            for k in range(32):
                m[br, k, :, mm * 128:(mm + 1) * 128] = ((bk == k) & valid).astype(np.float32)
    return m


def make_rope():
    t = np.arange(S)
    row = (t // 64).astype(np.float32)
    col = (t % 64).astype(np.float32)
    n = 16
    inv = (np.float32(10000.0) ** (-np.arange(n, dtype=np.float32) / np.float32(n))).astype(np.float32)
    ang = np.concatenate([row[:, None] * inv, col[:, None] * inv], axis=-1).astype(np.float32)
    return np.cos(ang).astype(np.float32), np.sin(ang).astype(np.float32)


def build(nseq=SEQ_PER_CORE, nl=NLAYER, dbg=False, stop=None):
    nc = bass.Bass("TRN2", target_bir_lowering=False)

    def din(name, shape, dtype=F32):
        return nc.dram_tensor(name, list(shape), dtype, kind="ExternalInput").ap()

    def dscr(name, shape, dtype):
        return nc.dram_tensor(name, list(shape), dtype, kind=("ExternalOutput" if dbg else "Internal")).ap()

    x_in = din("x", [nseq, S, D])
    y_out = nc.dram_tensor("y", [nseq, S, D], F32, kind="ExternalOutput").ap()
    w_in = din("w_in", [NLAYER, D, INW])
    w_out = din("w_out", [NLAYER, D, D])
    w_up = din("w_up", [NLAYER, D, 2 * DFF])
    w_down = din("w_down", [NLAYER, DFF, D])
    an_col = din("an_col", [NLAYER, 128, 8])
    fn_col = din("fn_col", [NLAYER, 128, 8])
    gq_rep = din("gq_rep", [128, NLAYER * 64])
    gk_rep = din("gk_rep", [128, NLAYER * 64])
    gfin_rep = din("gfin_rep", [128, D])
    relb_rep = din("relb_rep", [128, 256])
    convp = din("convp", [128, NLAYER * 44 * 4])
    ident_d = din("ident", [128, 128])
    cos_d = din("cos", [S, 32])
    sin_d = din("sin", [S, 32])
    masks_d = din("masks", [3, 32, 128, 256])

    xA = dscr("xA", [S, D], F32)
    xB = dscr("xB", [S, D], F32)
    qT = dscr("qT", [1024, S], BF16)
    kT = dscr("kT", [640, S], BF16)
    vd = dscr("vd", [PADV + S + PADV, 1280], BF16)
    oT = dscr("oT", [1024, S], BF16)
    winb = dscr("winb", [NLAYER, 128, 8, INW], BF16)
    woutb = dscr("woutb", [NLAYER, 128, 8, D], BF16)
    wupb = dscr("wupb", [NLAYER, NPAIR, 128, 8, 256], BF16)
    wdnb = dscr("wdnb", [NLAYER, 128, NPAIR, D], BF16)

    prog = Prog()
    pe = Q("pe", prog)
    act = Q("act", prog)
    dve = Q("dve", prog)
    pool = Q("pool", prog)
    sp = Q("sp", prog)
    queues = [pe, act, dve, pool, sp]
    pend = []

    ARENA_BYTES = 172 * 1024
    es = ExitStack()
    arena_t = es.enter_context(nc.sbuf_tensor("arena", [128, ARENA_BYTES // 2], BF16))
    identb = es.enter_context(nc.sbuf_tensor("identb", [128, 128], BF16))
    cos_t = es.enter_context(nc.sbuf_tensor("cos_t", [128, 32 * 32], F32))
    sin_t = es.enter_context(nc.sbuf_tensor("sin_t", [128, 32 * 32], F32))
    gq_t = es.enter_context(nc.sbuf_tensor("gq_t", [128, NLAYER * 64], F32))
    gk_t = es.enter_context(nc.sbuf_tensor("gk_t", [128, NLAYER * 64], F32))
    gfin_t = es.enter_context(nc.sbuf_tensor("gfin_t", [128, D], F32))
    convp_t = es.enter_context(nc.sbuf_tensor("convp_t", [128, NLAYER * 44 * 4], F32))
    ancol_t = es.enter_context(nc.sbuf_tensor("ancol_t", [128, NLAYER * 8], F32))
    fncol_t = es.enter_context(nc.sbuf_tensor("fncol_t", [128, NLAYER * 8], F32))
    E_t = es.enter_context(nc.sbuf_tensor("E_t", [128, 24 * 256], BF16))
    st_t = es.enter_context(nc.sbuf_tensor("st_t", [128, 256], F32))
    ps_t = es.enter_context(nc.psum_tensor("ps", [128, 4096], F32))

    A = Arena(arena_t[:], ARENA_BYTES)
    ident = identb[:]
    cosv = cos_t[:].rearrange("p (i c) -> p i c", c=32)
    sinv = sin_t[:].rearrange("p (i c) -> p i c", c=32)
    convv = convp_t[:].rearrange("p (l c f) -> p l c f", l=NLAYER, f=4)
    Ev = E_t[:]
    st = st_t[:]

    def bank(b, n=512):
        return ps_t[:, b * 512:b * 512 + n]

    def bank_bf(b, nb=1):
        return ps_t[:, b * 512:(b + nb) * 512].bitcast(BF16)

    def MM(out, lhsT, rhs, start, stop, sig=False):
        return pe.do(lambda e: e.matmul(out, lhsT=lhsT, rhs=rhs, start=start, stop=stop), sig)

    def TR(out, in_, sig=False):
        k = in_.shape[0]
        return pe.do(lambda e: e.transpose(out, in_, ident[0:k, 0:k]), sig)

    def ACT(out, in_, func, sig=False, **kw):
        return act.do(lambda e: e.activation(out, in_, func, **kw), sig)

    def TS(q, out, in0, s1, s2, op0, op1=None, sig=False):
        if op1 is None:
            return q.do(lambda e: e.tensor_scalar(out, in0, s1, None, op0), sig)
        return q.do(lambda e: e.tensor_scalar(out, in0, s1, s2, op0, op1), sig)

    def TT(q, out, in0, in1, op, sig=False):
        return q.do(lambda e: e.tensor_tensor(out, in0, in1, op), sig)

    def STT(q, out, in0, scalar, in1, op0, op1, sig=False):
        return q.do(lambda e: e.scalar_tensor_tensor(out, in0, scalar, in1, op0, op1), sig)

    def CP(q, out, in_, sig=False):
        return q.do(lambda e: e.tensor_copy(out, in_), sig)

    def MSET(q, ap, val, sig=False):
        return q.do(lambda e: e.memset(ap, val), sig)

    def barrier():
        tks = [q.last for q in queues if q.last is not None] + list(pend)
        for q in queues:
            q.wait(*tks)
        pend.clear()

    def rstd_chain(ssq_ap, tmp1, tmp2, out_ap, inv_n, wait_tk):
        dve.wait(wait_tk)
        t = TS(dve, tmp1, ssq_ap, inv_n, EPS, ALU.mult, ALU.add, sig=True)
        act.wait(t)
        t = ACT(tmp2, tmp1, AF.Ln, sig=True)
        act.wait(t)
        return ACT(out_ap, tmp2, AF.Exp, sig=True, scale=-0.5)

    def setup():
        A.reset()
        idf = A.f32(128)
        eb = A.f32(256)
        ebr = A.f32(256)
        mk = [A.f32(32 * 256) for _ in range(2)]
        eacc = [A.f32(256) for _ in range(2)]
        s0 = Slot(prog)
        sp.dma(idf, ident_d, s0)
        sp.dma(cosv, cos_d.rearrange("(i p) c -> p i c", p=128), s0)
        sp.dma(sinv, sin_d.rearrange("(i p) c -> p i c", p=128), s0)
        sp.dma(gq_t[:], gq_rep, s0)
        sp.dma(gk_t[:], gk_rep, s0)
        sp.dma(gfin_t[:], gfin_rep, s0)
        sp.dma(convp_t[:], convp, s0)
        sp.dma(ancol_t[:].rearrange("p (l k) -> p l k", l=NLAYER), an_col.rearrange("l p k -> p l k"), s0)
        sp.dma(fncol_t[:].rearrange("p (l k) -> p l k", l=NLAYER), fn_col.rearrange("l p k -> p l k"), s0)
        t0 = sp.dma(ebr, relb_rep, s0)
        dve.wait(t0)
        act.wait(t0)
        pool.wait(t0)
        t_id = CP(dve, ident, idf, sig=True)
        t_eb = ACT(eb, ebr, AF.Exp, sig=True)
        dve.wait(t_eb)
        pool.wait(t_eb)
        ms = [Slot(prog), Slot(prog)]
        tk_m = {}
        tk_use = {}
        for br in range(3):
            sp.wait(tk_use.get(br - 2))
            tk_m[br] = sp.dma(mk[br % 2].rearrange("p (k c) -> p k c", k=32),
                              masks_d[br].rearrange("k p c -> p k c"), ms[br % 2])
            last = []
            for h in range(8):
                q = dve
                q.wait(tk_m[br])
                ea = eacc[h % 2]
                mv = mk[br % 2].rearrange("p (k c) -> p k c", k=32)
                for k in range(32):
                    col = eb[:, k * 8 + h:k * 8 + h + 1]
                    if k == 0:
                        TS(q, ea, mv[:, k, :], col, None, ALU.mult)
                    else:
                        STT(q, ea, mv[:, k, :], col, ea, ALU.mult, ALU.add)
                t = CP(q, Ev[:, (h * 3 + br) * 256:(h * 3 + br + 1) * 256], ea, sig=True)
                if h >= 6:
                    last.append(t)
            sp.wait(*last)
            tk_use[br] = last[-1]
        z = A.bf(1280)
        t = MSET(pool, z, 0.0, sig=True)
        pool.wait(t)
        zs = Slot(prog)
        for j in range(PADV // 128):
            pend.append(pool.dma(vd[j * 128:(j + 1) * 128, :], z, zs))
            pend.append(pool.dma(vd[PADV + S + j * 128:PADV + S + (j + 1) * 128, :], z, zs))
        barrier()

    def prepass():
        A.reset()
        stg = [A.f32(5632) for _ in range(2)]
        ob = [A.bf(5632) for _ in range(2)]
        ls = [Slot(prog), Slot(prog)]
        ss = [Slot(prog), Slot(prog)]
        items = []
        for l in range(nl):
            for k in range(8):
                items.append((w_in[l, k * 128:(k + 1) * 128, :], INW, ancol_t[:, l * 8 + k:l * 8 + k + 1],
                              lambda o, l=l, k=k: [(winb[l, :, k, :], o)]))
            for k in range(8):
                items.append((w_out[l, k * 128:(k + 1) * 128, :], D, None,
                              lambda o, l=l, k=k: [(woutb[l, :, k, :], o)]))
            for k in range(8):
                items.append((w_up[l, k * 128:(k + 1) * 128, :], 2 * DFF, fncol_t[:, l * 8 + k:l * 8 + k + 1],
                              lambda o, l=l, k=k: [
                                  (wupb[l].rearrange("c p k (g j) -> p k g c j", g=2)[:, k, g],
                                   o[:, g * DFF:(g + 1) * DFF].rearrange("p (c j) -> p c j", j=128))
                                  for g in range(2)]))
            for c in range(NPAIR):
                items.append((w_down[l, c * 128:(c + 1) * 128, :], D, None,
                              lambda o, l=l, c=c: [(wdnb[l, :, c, :], o)]))
        tk_ld = {}
        tk_cv = {}
        tk_st = {}

        def load(i):
            src, n, _, _ = items[i]
            sp.wait(tk_cv.get(i - 2))
            tk_ld[i] = sp.dma(stg[i % 2][:, 0:n], src, ls[i % 2])

        load(0)
        for i in range(len(items)):
            if i + 1 < len(items):
                load(i + 1)
            src, n, sc, dstf = items[i]
            q = dve if i % 2 == 0 else act
            q.wait(tk_ld[i], tk_st.get(i - 2))
            if q is dve:
                if sc is None:
                    tk_cv[i] = CP(dve, ob[i % 2][:, 0:n], stg[i % 2][:, 0:n], sig=True)
                else:
                    tk_cv[i] = TS(dve, ob[i % 2][:, 0:n], stg[i % 2][:, 0:n], sc, None, ALU.mult, sig=True)
            else:
                if sc is None:
                    tk_cv[i] = ACT(ob[i % 2][:, 0:n], stg[i % 2][:, 0:n], AF.Copy, sig=True)
                else:
                    tk_cv[i] = ACT(ob[i % 2][:, 0:n], stg[i % 2][:, 0:n], AF.Copy, sig=True, scale=sc)
            pool.wait(tk_cv[i])
            for d_ap, s_ap in dstf(ob[i % 2][:, 0:n]):
                tk_st[i] = pool.dma(d_ap, s_ap, ss[i % 2])
        pend.extend(tk_st.values())
        barrier()

    slotW = Slot(prog)
    _slots = {}

    def SL(name):
        if name not in _slots:
            _slots[name] = Slot(prog)
        return _slots[name]

    def phaseP(s, l):
        A.reset()
        xt = [A.f32(1024) for _ in range(2)]
        junk = A.bf(1024)
        hb = A.bf(1024)
        hT = A.bf(1024)
        win = A.bf(8 * INW)
        tq = A.f32(640)
        sq = A.f32(640)
        r1 = A.f32(320)
        r2 = A.f32(320)
        G = A.f32(640)
        qk = [A.bf(1664) for _ in range(2)]
        vt = [A.bf(1280) for _ in range(2)]
        stg = [A.bf(13 * 512) for _ in range(2)]
        barrier()
        xsrc = x_in[s] if l == 0 else xA
        winv = win.rearrange("p (k n) -> p k n", k=8)
        tw = sp.dma(winv, winb[l], slotW)
        Gv = G.rearrange("p (h c) -> p h c", h=10)
        CP(pool, Gv[:, 0:8, :], gq_t[:, l * 64:(l + 1) * 64].unsqueeze(1).to_broadcast([128, 8, 64]))
        CP(pool, Gv[:, 8:10, :], gk_t[:, l * 64:(l + 1) * 64].unsqueeze(1).to_broadcast([128, 2, 64]))
        for b in range(2):
            tG = MSET(pool, vt[b].rearrange("p (h e) -> p h e", h=10)[:, :, 64:128], 1.0, sig=True)
        dve.wait(tG)
        psT = bank_bf(0)
        psQ = bank_bf(6, 2)
        groups = ((0, 512, 1), (512, 768, 2), (768, 1280, 3), (1280, 1792, 4), (1792, 2304, 5))
        sx = [SL("P.sx0"), SL("P.sx1")]
        sv = [SL("P.sv0"), SL("P.sv1")]
        sg = [SL("P.sg0"), SL("P.sg1")]
        tk_x, tk_sq, tk_hb, tk_tr, tk_hT, tk_proj = {}, {}, {}, {}, {}, {}
        tk_eva, tk_evd, tk_rope, tk_tr2, tk_stg, tk_sd, tk_vst = {}, {}, {}, {}, {}, {}, {}

        def load(i):
            sp.wait(tk_hb.get(i - 2), tk_sq.get(i - 2))
            tk_x[i] = sp.dma(xt[i % 2], xsrc[i * 128:(i + 1) * 128, :], sx[i % 2])

        load(0)
        for i in range(32):
            if i + 1 < 32:
                load(i + 1)
            sl = i % 2
            c0 = (i % 4) * 48

            def c(j, n=1):
                return st[:, c0 + j:c0 + j + n]
            act.wait(tk_x[i])
            tk_sq[i] = ACT(junk, xt[sl], AF.Square, sig=True, accum_out=c(0))
            t_rs = rstd_chain(c(0), c(1), c(2), c(3), 1.0 / D, tk_sq[i])
            dve.wait(t_rs, tk_tr.get(i - 1))
            tk_hb[i] = TS(dve, hb, xt[sl], c(3), None, ALU.mult, sig=True)
            pe.wait(tk_hb[i], tk_hT.get(i - 1))
            for k in range(8):
                t = TR(psT[:, k * 128:(k + 1) * 128], hb[:, k * 128:(k + 1) * 128], sig=(k == 7))
            tk_tr[i] = t
            dve.wait(tk_tr[i], tk_proj.get(i - 1))
            tk_hT[i] = CP(dve, hT, psT, sig=True)
            pe.wait(tk_hT[i], tw, tk_eva.get(i - 1), tk_evd.get(i - 1))
            for k in range(8):
                for gi, (a0, a1, bk) in enumerate(groups):
                    t = MM(bank(bk, a1 - a0), hT[:, k * 128:(k + 1) * 128], winv[:, k, a0:a1], k == 0, k == 7,
                           sig=(k == 7 and gi == 4))
            tk_proj[i] = t
            act.wait(tk_proj[i], tk_rope.get(i - 1))
            ACT(tq[:, 0:512], bank(1), AF.Copy)
            t_tq = ACT(tq[:, 512:640], bank(2, 128), AF.Copy, sig=True)
            qkc = qk[i % 2]
            act.wait(tk_tr2.get(i - 2))
            ACT(qkc[:, 640:1152], bank(3), AF.Copy)
            t_qb = ACT(qkc[:, 1152:1664], bank(4), AF.Copy, sig=True)
            dve.wait(t_tq)
            tq3 = tq.rearrange("p (h c) -> p h c", h=10)
            TT(dve, sq, tq, tq, ALU.mult)
            t_hs = dve.do(lambda e, o=c(4, 10), i_=sq.rearrange("p (h c) -> p h c", h=10): e.tensor_reduce(
                o, i_, AX.X, ALU.add), sig=True)
            t_rsh = rstd_chain(c(4, 10), c(14, 10), c(24, 10), c(34, 10), 1.0 / 64, t_hs)
            tk_eva[i] = t_rsh
            dve.wait(t_rsh, tk_tr2.get(i - 2))
            TT(dve, tq3, tq3, c(34, 10).unsqueeze(2).to_broadcast([128, 10, 64]), ALU.mult)
            TT(dve, tq3, tq3, Gv, ALU.mult)
            x0 = tq3[:, :, 0:64:2]
            x1 = tq3[:, :, 1:64:2]
            cb = cosv[:, i, :].unsqueeze(1).to_broadcast([128, 10, 32])
            sb = sinv[:, i, :].unsqueeze(1).to_broadcast([128, 10, 32])
            r1v = r1.rearrange("p (h c) -> p h c", h=10)
            r2v = r2.rearrange("p (h c) -> p h c", h=10)
            qk3 = qkc[:, 0:640].rearrange("p (h c) -> p h c", h=10)
            TT(dve, r1v, x0, cb, ALU.mult)
            TT(dve, r2v, x1, sb, ALU.mult)
            TT(dve, qk3[:, :, 0:64:2], r1v, r2v, ALU.subtract)
            TT(dve, r1v, x0, sb, ALU.mult)
            TT(dve, r2v, x1, cb, ALU.mult)
            tk_rope[i] = TT(dve, qk3[:, :, 1:64:2], r1v, r2v, ALU.add, sig=True)
            dve.wait(tk_vst.get(i - 2))
            vt3 = vt[sl].rearrange("p (h e) -> p h e", h=10)
            CP(dve, vt3[:, 0:2, 0:64], bank(2, 256)[:, 128:256].rearrange("p (h e) -> p h e", h=2))
            tk_evd[i] = CP(dve, vt3[:, 2:10, 0:64], bank(5).rearrange("p (h e) -> p h e", h=8), sig=True)
            pool.wait(tk_evd[i])
            tk_vst[i] = pool.dma(vd[PADV + i * 128:PADV + (i + 1) * 128, :], vt[sl], sv[sl])
            pend.append(tk_vst[i])
            pe.wait(tk_rope[i], t_qb, tk_stg.get(i - 1))
            for j in range(13):
                t = TR(psQ[:, j * 128:(j + 1) * 128], qkc[:, j * 128:(j + 1) * 128], sig=(j == 12))
            tk_tr2[i] = t
            g = i // 4
            dve.wait(tk_tr2[i], tk_sd.get(g - 2))
            sg3 = stg[g % 2].rearrange("p (j t) -> p j t", j=13)
            tk_stg[i] = CP(dve, sg3[:, :, (i % 4) * 128:(i % 4 + 1) * 128],
                           psQ[:, 0:1664].rearrange("p (j t) -> p j t", j=13), sig=True)
            if i % 4 == 3:
                pool.wait(tk_stg[i])
                t0 = g * 512
                pool.dma(qT[0:512, t0:t0 + 512].rearrange("(j p) t -> p j t", p=128), sg3[:, 0:4, :], sg[g % 2])
                pool.dma(kT[0:128, t0:t0 + 512], sg3[:, 4, :], sg[g % 2])
                pool.dma(qT[512:1024, t0:t0 + 512].rearrange("(j p) t -> p j t", p=128), sg3[:, 5:9, :], sg[g % 2])
                tk_sd[g] = pool.dma(kT[128:640, t0:t0 + 512].rearrange("(j p) t -> p j t", p=128), sg3[:, 9:13, :],
                                    sg[g % 2])
                pend.append(tk_sd[g])

    def phaseATT(s, l):
        A.reset()
        qh = [A.bf(S) for _ in range(2)]
        kh = [A.bf(PADK + S + PADK) for _ in range(2)]
        vh = [A.bf(NVCH * 128) for _ in range(2)]
        pT = [A.bf(1024) for _ in range(3)]
        pex = [A.f32(512) for _ in range(3)]
        pTb = [A.bf(512) for _ in range(3)]
        acc = A.f32(S)
        oh = [A.bf(S) for _ in range(2)]
        oh2 = [A.bf(S) for _ in range(2)]
        rd = A.f32(512)
        barrier()
        for b in range(2):
            MSET(pool, kh[b][:, 0:PADK], 0.0)
            tkz = MSET(pool, kh[b][:, PADK + S:PADK + S + PADK], 0.0, sig=True)
        sp.wait(tkz)
        pe.wait(tkz)
        sh = [SL("A.sh0"), SL("A.sh1")]
        so = [SL("A.so0"), SL("A.so1")]
        heads = [("A", i) for i in range(4)] + [("B", h) for h in range(8)]
        if DBG_HEADS is not None:
            heads = DBG_HEADS
        tk_head, tk_done, tk_ohst = {}, {}, {}
        psS2 = [ps_t[:, 0:1024], ps_t[:, 1024:2048]]
        psOa = [bank(4), bank(6)]
        psOb = [bank(5), bank(7)]
        psSB = [bank(1), bank(2), bank(3)]
        psOB = [bank(4, 256), bank(5, 256), bank(6, 256)]
        cnt = {"c": 0, "q": 0, "b": 0}
        tk_qk, tk_exp, tk_pv, tk_norm = {}, {}, {}, {}
        tk_s1, tk_eB, tk_mB, tk_pvB, tk_accB = {}, {}, {}, {}, {}

        def loadhead(n):
            typ, h = heads[n]
            b = n % 2
            sp.wait(tk_done.get(n - 2))
            if typ == "A":
                g = h // 2
                sp.dma(qh[b][:, :], qT[h * 128:(h + 1) * 128, :], sh[b])
                sp.dma(kh[b][0:64, PADK:PADK + S], kT[g * 64:(g + 1) * 64, :], sh[b])
                sp.dma(kh[b][64:128, PADK:PADK + S], kT[g * 64:(g + 1) * 64, :], sh[b])
                tk = sp.dma(vh[b][:, 0:32 * 128].rearrange("p (m e) -> p m e", e=128),
                            vd[PADV:PADV + S, g * 128:(g + 1) * 128].rearrange("(m p) e -> p m e", p=128), sh[b])
            else:
                sp.dma(qh[b][0:64, :], qT[512 + h * 64:512 + (h + 1) * 64, :], sh[b])
                sp.dma(kh[b][0:64, PADK:PADK + S], kT[128 + h * 64:128 + (h + 1) * 64, :], sh[b])
                for (d, base, nm) in BRANCHES:
                    for r in range(d):
                        r0 = PADV - 64 * d + r
                        src = vd[r0:r0 + (128 * nm - 1) * d + 1:d, (2 + h) * 128:(3 + h) * 128]
                        tk = sp.dma(vh[b][:, (base + r * nm) * 128:(base + (r + 1) * nm) * 128].rearrange(
                            "p (m e) -> p m e", e=128), src.rearrange("(m p) e -> p m e", p=128), sh[b])
            tk_head[n] = tk

        def normalize(src_num, src_den, dst, sig):
            dve.do(lambda e: e.reciprocal(rd[64:128, :], src_den))
            CP(dve, rd[0:64, :], rd[64:128, :])
            return TT(dve, dst, src_num, rd[0:64, :], ALU.mult, sig=sig)

        def headA(n, i):
            b = n % 2
            jobs = [(qt, c) for qt in range(8) for c in range(32)]
            base = cnt["c"]
            qbase = cnt["q"]

            def QK(j):
                qt, c = jobs[j]
                gi = base + j
                sl = gi % 2
                pe.wait(tk_head[n], tk_exp.get(gi - 2))
                if j == 0:
                    pe.wait(dve.last)
                MM(psS2[sl][:, 0:512], kh[b][0:64, PADK + c * 128:PADK + (c + 1) * 128],
                   qh[b][0:64, qt * 512:(qt + 1) * 512], True, True)
                tk_qk[gi] = MM(psS2[sl][:, 512:1024], kh[b][64:128, PADK + c * 128:PADK + (c + 1) * 128],
                               qh[b][64:128, qt * 512:(qt + 1) * 512], True, True, sig=True)

            QK(0)
            for j, (qt, c) in enumerate(jobs):
                gi = base + j
                gq = qbase + qt
                if j + 1 < len(jobs):
                    QK(j + 1)
                act.wait(tk_qk[gi], tk_pv.get(gi - 3))
                tk_exp[gi] = ACT(pT[gi % 3], psS2[gi % 2], AF.Exp, sig=True, scale=0.125)
                pe.wait(tk_exp[gi])
                if c == 0:
                    pe.wait(tk_norm.get(gq - 2))
                MM(psOa[gq % 2], vh[b][:, c * 128:(c + 1) * 128], pT[gi % 3][:, 0:512], c == 0, c == 31)
                tk_pv[gi] = MM(psOb[gq % 2], vh[b][:, c * 128:(c + 1) * 128], pT[gi % 3][:, 512:1024], c == 0,
                               c == 31, sig=True)
                if c == 31:
                    dve.wait(tk_pv[gi], tk_ohst.get(n - 2))
                    normalize(psOa[gq % 2][0:64, :], psOa[gq % 2][64:128, :],
                              oh[b][0:64, qt * 512:(qt + 1) * 512], False)
                    tk_norm[gq] = normalize(psOb[gq % 2][0:64, :], psOb[gq % 2][64:128, :],
                                            oh2[b][0:64, qt * 512:(qt + 1) * 512], True)
            cnt["c"] += len(jobs)
            cnt["q"] += 8
            tk_done[n] = tk_pv[base + len(jobs) - 1]
            pool.wait(tk_norm[qbase + 7])
            pool.dma(oT[i * 128:i * 128 + 64, :], oh[b][0:64, :], so[b])
            tk_ohst[n] = pool.dma(oT[i * 128 + 64:i * 128 + 128, :], oh2[b][0:64, :], so[b])
            pend.append(tk_ohst[n])

        def headB(n, h):
            b = n % 2
            items = []
            for br, (d, vbase, nm) in enumerate(BRANCHES):
                if DBG_LEVEL < 9 and br != DBG_LEVEL:
                    continue
                for r in range(d):
                    for bp in range(16 // d):
                        items.append((br, d, vbase, nm, r, bp))
            base = cnt["b"]
            if n == 0 or heads[n - 1][0] == "A":
                pe.wait(act.last, dve.last)

            def S1(j):
                br, d, vbase, nm, r, bp = items[j]
                gi = base + j
                pe.wait(tk_head[n], tk_eB.get(gi - 3))
                bk = 2 * bp
                q0 = r + d * 128 * bk
                kc = [PADK + r + d * (128 * (bk + m) - 64) for m in range(3)]
                MM(psSB[gi % 3][:, 0:128], kh[b][0:64, kc[0]:kc[0] + 127 * d + 1:d],
                   qh[b][0:64, q0:q0 + 127 * d + 1:d], True, True)
                MM(psSB[gi % 3][:, 128:384], kh[b][0:64, kc[1]:kc[1] + 127 * d + 1:d],
                   qh[b][0:64, q0:q0 + 255 * d + 1:d], True, True)
                t = MM(psSB[gi % 3][:, 384:512], kh[b][0:64, kc[2]:kc[2] + 127 * d + 1:d],
                       qh[b][0:64, q0 + 128 * d:q0 + 255 * d + 1:d], True, True, sig=True)
                tk_s1[gi] = t

            def EXPMUL(j):
                br, d, vbase, nm, r, bp = items[j]
                gi = base + j
                act.wait(tk_s1[gi], tk_mB.get(gi - 3))
                tk_eB[gi] = ACT(pex[gi % 3], psSB[gi % 3], AF.Exp, sig=True, scale=0.125)
                dve.wait(tk_eB[gi], tk_pvB.get(gi - 3))
                Eb = Ev[:, (h * 3 + br) * 256:(h * 3 + br + 1) * 256].unsqueeze(1).to_broadcast([128, 2, 256])
                tk_mB[gi] = TT(dve, pTb[gi % 3].rearrange("p (a c) -> p a c", a=2),
                               pex[gi % 3].rearrange("p (a c) -> p a c", a=2), Eb, ALU.mult, sig=True)

            S1(0)
            if len(items) > 1:
                S1(1)
            EXPMUL(0)
            for j, (br, d, vbase, nm, r, bp) in enumerate(items):
                gi = base + j
                if j + 2 < len(items):
                    S1(j + 2)
                if j + 1 < len(items):
                    EXPMUL(j + 1)
                pe.wait(tk_mB[gi], tk_accB.get(gi - 3))
                ch = vbase + r * nm + 2 * bp
                MM(psOB[gi % 3][:, 0:256], vh[b][:, (ch + 1) * 128:(ch + 2) * 128], pTb[gi % 3][:, 128:384], True, False)
                MM(psOB[gi % 3][:, 0:128], vh[b][:, ch * 128:(ch + 1) * 128], pTb[gi % 3][:, 0:128], False, True)
                t = MM(psOB[gi % 3][:, 128:256], vh[b][:, (ch + 2) * 128:(ch + 3) * 128], pTb[gi % 3][:, 384:512],
                       False, True, sig=True)
                tk_pvB[gi] = t
                dve.wait(t)
                q0 = r + d * 256 * bp
                accv = acc[:, q0:q0 + 255 * d + 1:d]
                if br == 0:
                    tk_accB[gi] = CP(dve, accv, psOB[gi % 3], sig=True)
                else:
                    tk_accB[gi] = TT(dve, accv, accv, psOB[gi % 3], ALU.add, sig=True)
            cnt["b"] += len(items)
            tk_done[n] = tk_pvB[base + len(items) - 1]
            dve.wait(tk_ohst.get(n - 2))
            for qt in range(8):
                t = normalize(acc[0:64, qt * 512:(qt + 1) * 512], acc[64:128, qt * 512:(qt + 1) * 512],
                              oh[b][0:64, qt * 512:(qt + 1) * 512], qt == 7)
            pool.wait(t)
            tk_ohst[n] = pool.dma(oT[512 + h * 64:512 + (h + 1) * 64, :], oh[b][0:64, :], so[b])
            pend.append(tk_ohst[n])

        loadhead(0)
        for n, (typ, h) in enumerate(heads):
            if n + 1 < len(heads):
                loadhead(n + 1)
            if typ == "A":
                headA(n, h)
            else:
                headB(n, h)

    def phaseO(s, l):
        A.reset()
        wout = A.bf(8 * D)
        ot = [A.bf(8 * 512) for _ in range(2)]
        xt = [A.f32(D) for _ in range(3)]
        barrier()
        woutv = wout.rearrange("p (k n) -> p k n", k=8)
        tw = sp.dma(woutv, woutb[l], slotW)
        xsrc = x_in[s] if l == 0 else xA
        sot = [SL("O.sot0"), SL("O.sot1")]
        sx = [SL("O.sx%d" % i) for i in range(3)]
        sxs = [SL("O.sxs%d" % i) for i in range(3)]
        psX = [ps_t[:, 0:1024], ps_t[:, 1024:2048]]
        tk_ot, tk_x, tk_mm, tk_add, tk_xst = {}, {}, {}, {}, {}
        for g in range(8):
            sp.wait(tk_mm.get((g - 2) * 4 + 3))
            otv = ot[g % 2].rearrange("p (k t) -> p k t", k=8)
            tk_ot[g] = sp.dma(otv, oT[:, g * 512:(g + 1) * 512].rearrange("(k p) t -> p k t", p=128), sot[g % 2])
            for j in range(4):
                i = g * 4 + j
                sp.wait(tk_xst.get(i - 3))
                tk_x[i] = sp.dma(xt[i % 3], xsrc[i * 128:(i + 1) * 128, :], sx[i % 3])
                pe.wait(tk_ot[g], tw, tk_add.get(i - 2))
                for n2 in range(2):
                    for k in range(8):
                        t = MM(psX[i % 2][:, n2 * 512:(n2 + 1) * 512], otv[:, k, j * 128:(j + 1) * 128],
                               woutv[:, k, n2 * 512:(n2 + 1) * 512], k == 0, k == 7, sig=(n2 == 1 and k == 7))
                tk_mm[i] = t
                dve.wait(t, tk_x[i])
                TT(dve, xt[i % 3][:, 0:512], xt[i % 3][:, 0:512], psX[i % 2][:, 0:512], ALU.add)
                tk_add[i] = TT(dve, xt[i % 3][:, 512:1024], xt[i % 3][:, 512:1024], psX[i % 2][:, 512:1024], ALU.add,
                               sig=True)
                pool.wait(tk_add[i])
                tk_xst[i] = pool.dma(xB[i * 128:(i + 1) * 128, :], xt[i % 3], sxs[i % 3])
                pend.append(tk_xst[i])

    def phaseF(s, l, last):
        A.reset()
        xt5 = [A.f32(5 * D) for _ in range(2)]
        junk = A.bf(D)
        hb = A.bf(5 * D)
        h2T = A.bf(8 * 520)
        wup = [A.bf(8 * 256) for _ in range(3)]
        wdn = A.bf(NPAIR * D)
        U = [[A.f32(516) for _ in range(2)] for _ in range(2)]
        cg = [A.f32(512) for _ in range(2)]
        sgl = [A.f32(512) for _ in range(2)]
        cv = A.f32(512)
        aT = A.bf(NPAIR * 512)
        barrier()
        wdnv = wdn.rearrange("p (c n) -> p c n", c=NPAIR)
        twd = sp.dma(wdnv, wdnb[l], slotW)
        h2Tv = h2T.rearrange("p (k t) -> p k t", k=8)
        aTv = aT.rearrange("p (c t) -> p c t", c=NPAIR)
        hb3 = hb.rearrange("p (j d) -> p j d", j=5)
        sxl = [SL("F.sxl0"), SL("F.sxl1")]
        sws = [SL("F.sws%d" % i) for i in range(3)]
        sxs = [SL("F.sxs0"), SL("F.sxs1")]
        psT = [bank_bf(0), bank_bf(1)]
        psHs = [bank(0, 16), bank(1, 16)]
        psGV = [(bank(2), bank(3)), (bank(4), bank(5))]
        psD = [ps_t[:, 3072:4096], ps_t[:, 1024:2048]]
        dst = y_out[s] if last else xA
        tk_x, tk_st, tk_w, tk_pair, tk_ev, tk_a, tk_dn, tk_h2 = {}, {}, {}, {}, {}, {}, {}, {}
        gpc = {"p": 0, "d": 0, "t": 0}

        def loadx(T):
            slx = T % 2
            t0 = T * 512
            x5 = xt5[slx].rearrange("p (j d) -> p j d", j=5)
            pool.wait(*tk_st.get(T - 2, ()))
            tz = MSET(pool, x5[0:2, 4, :], 0.0, sig=True)
            sp.wait(tz)
            sp.wait(*tk_st.get(T - 2, ()))
            tk = sp.dma(x5[:, 0:4, :], xB[t0:t0 + 512, :].rearrange("(j p) d -> p j d", p=128), sxl[slx])
            if T > 0:
                tk = sp.dma(x5[0:1, 4, :], xB[t0 - 1:t0, :], sxl[slx])
            if T < 7:
                tk = sp.dma(x5[1:2, 4, :], xB[t0 + 512:t0 + 513, :], sxl[slx])
            tk_x[T] = tk

        def loadw(T, c):
            gi = T * NPAIR + c
            sp.wait(tk_pair.get(gi - 3))
            tk_w[gi] = sp.dma(wup[gi % 3].rearrange("p (k n) -> p k n", k=8), wupb[l, c], sws[gi % 3])

        loadx(0)
        for T in range(8):
            if T + 1 < 8:
                loadx(T + 1)
            slx = T % 2
            x5 = xt5[slx].rearrange("p (j d) -> p j d", j=5)
            c0 = (T % 2) * 64

            def c(j, n=1):
                return st[:, c0 + j:c0 + j + n]
            loadw(T, 0)
            loadw(T, 1)
            act.wait(tk_x[T])
            for j in range(5):
                npp = 128 if j < 4 else 2
                t = ACT(junk[0:npp, :], x5[0:npp, j, :], AF.Square, sig=(j == 4), accum_out=st[0:npp, c0 + j:c0 + j + 1])
            t_rs = rstd_chain(c(0, 5), c(8, 5), c(16, 5), c(24, 5), 1.0 / D, t)
            dve.wait(t_rs, tk_h2.get(T - 1))
            for j in range(5):
                npp = 128 if j < 4 else 2
                t = TS(dve, hb3[0:npp, j, :], x5[0:npp, j, :], st[0:npp, c0 + 24 + j:c0 + 25 + j], None, ALU.mult,
                       sig=True)
            t_hb = t
            pe.wait(t_hb, tk_dn.get(T - 1), tk_dn.get(("d", gpc["d"] - 1)), tk_dn.get(("d", gpc["d"] - 2)),
                    *tk_ev.get(T - 1, ()))
            tcs = []
            for j in range(5):
                gt = gpc["t"]
                gpc["t"] += 1
                npp = 128 if j < 4 else 2
                pe.wait(tk_h2.get(("c", gt - 2)))
                for k in range(8):
                    t = TR(psT[gt % 2][:, k * 128:k * 128 + npp], hb3[0:npp, j, k * 128:(k + 1) * 128], sig=(k == 7))
                dve.wait(t)
                if j < 4:
                    t = CP(dve, h2Tv[:, :, j * 128:(j + 1) * 128],
                           psT[gt % 2].rearrange("p (k t) -> p k t", k=8), sig=True)
                else:
                    t = CP(dve, h2Tv[:, :, 512:514],
                           psT[gt % 2].rearrange("p (k t) -> p k t", k=8)[:, :, 0:2], sig=True)
                tk_h2[("c", gt)] = t
            t_h2 = t
            tk_ev[T] = []
            for cpi in range(NPAIR):
                gi = T * NPAIR + cpi
                if cpi + 2 < NPAIR:
                    loadw(T, cpi + 2)
                par = gi % 2
                wv = wup[gi % 3].rearrange("p (k n) -> p k n", k=8)
                pG, pV = psGV[par]
                hoff = 0
                psH = psHs[par]
                pe.wait(tk_w[gi], t_h2, tk_a.get(gi - 2))
                for (pp, n0, ho) in ((pG, 0, hoff), (pV, 128, hoff + 2)):
                    for k in range(8):
                        MM(pp, wv[:, k, n0:n0 + 128], h2Tv[:, k, 0:512], k == 0, k == 7)
                        t = MM(psH[:, ho:ho + 2], wv[:, k, n0:n0 + 128], h2Tv[:, k, 512:514], k == 0, k == 7,
                               sig=(k == 7 and n0 == 128))
                tk_pair[gi] = t
                Ug, Uv = U[par]
                act.wait(t, tk_a.get(gi - 2))
                ACT(Ug[:, 1:513], pG, AF.Copy)
                ACT(Ug[:, 0:514:513], psH[:, hoff:hoff + 2], AF.Copy)
                ACT(Uv[:, 1:513], pV, AF.Copy)
                t_ev = ACT(Uv[:, 0:514:513], psH[:, hoff + 2:hoff + 4], AF.Copy, sig=True)
                tk_ev[T] = [t_ev]
                dve.wait(t_ev)
                cgp = cg[par]
                for (Ux, co, dstc) in ((Ug, cpi, cgp), (Uv, NPAIR + cpi, cv)):
                    TS(dve, dstc, Ux[:, 1:513], convv[:, l, co, 1:2], convv[:, l, co, 3:4], ALU.mult, ALU.add)
                    STT(dve, dstc, Ux[:, 0:512], convv[:, l, co, 0:1], dstc, ALU.mult, ALU.add)
                    t = STT(dve, dstc, Ux[:, 2:514], convv[:, l, co, 2:3], dstc, ALU.mult, ALU.add, sig=(Ux is Ug))
                    if Ux is Ug:
                        t_cg = t
                act.wait(t_cg)
                t_sg = ACT(sgl[par], cgp, AF.Silu, sig=True)
                dve.wait(t_sg)
                if cpi == 0:
                    dve.wait(tk_dn.get(T - 1))
                tk_a[gi] = TT(dve, aTv[:, cpi, :], sgl[par], cv, ALU.mult, sig=True)
            tk_st[T] = []
            for s4 in range(4):
                gd = gpc["d"]
                gpc["d"] += 1
                pD = psD[gd % 2]
                pe.wait(tk_a[T * NPAIR + NPAIR - 1], twd, tk_dn.get(("d", gd - 2)))
                for n2 in range(2):
                    for cc in range(NPAIR):
                        t = MM(pD[:, n2 * 512:(n2 + 1) * 512], aTv[:, cc, s4 * 128:(s4 + 1) * 128],
                               wdnv[:, cc, n2 * 512:(n2 + 1) * 512], cc == 0, cc == NPAIR - 1,
                               sig=(n2 == 1 and cc == NPAIR - 1))
                tk_dn[T] = t
                dve.wait(t)
                xs = x5[:, s4, :]
                TT(dve, xs[:, 0:512], xs[:, 0:512], pD[:, 0:512], ALU.add)
                t = TT(dve, xs[:, 512:1024], xs[:, 512:1024], pD[:, 512:1024], ALU.add, sig=True)
                tk_dn[("d", gd)] = t
                if last:
                    cc0 = c0 + 32 + s4 * 4
                    act.wait(t)
                    t = ACT(junk, xs, AF.Square, sig=True, accum_out=st[:, cc0:cc0 + 1])
                    t = rstd_chain(st[:, cc0:cc0 + 1], st[:, cc0 + 1:cc0 + 2], st[:, cc0 + 2:cc0 + 3],
                                   st[:, cc0 + 3:cc0 + 4], 1.0 / D, t)
                    dve.wait(t)
                    t = STT(dve, xs, xs, st[:, cc0 + 3:cc0 + 4], gfin_t[:], ALU.mult, ALU.mult, sig=True)
                pool.wait(t)
                r0 = T * 512 + s4 * 128
                tks = pool.dma(dst[r0:r0 + 128, :], xs, sxs[slx])
                tk_st[T].append(tks)
                pend.append(tks)

    setup()
    if stop != "setup":
        prepass()
    for s in range(nseq):
        for l in range(nl):
            if stop in ("setup", "prepass"):
                break
            phaseP(s, l)
            if stop == "P":
                break
            phaseATT(s, l)
            if stop == "ATT":
                break
            phaseO(s, l)
            if stop == "O":
                break
            phaseF(s, l, l == nl - 1)
    barrier()

    for so in prog.sems:
        so.h = es.enter_context(nc.semaphore())
    with nc.Block() as block:
        @block.tensor
        def _(e):
            pe.replay(e)

        @block.scalar
        def _(e):
            act.replay(e)

        @block.vector
        def _(e):
            dve.replay(e)

        @block.gpsimd
        def _(e):
            pool.replay(e)

        @block.sync
        def _(e):
            sp.replay(e)
    es.close()
    return nc


_CONST = {}


def host_consts():
    if not _CONST:
        cos, sin = make_rope()
        _CONST["cos"] = cos
        _CONST["sin"] = sin
        _CONST["masks"] = make_masks()
        _CONST["ident"] = np.eye(128, dtype=np.float32)
    return _CONST


def shared_inputs(attn_norm, w_in, q_norm, k_norm, rel_bias, w_out, ffn_norm, w_up, conv_w, conv_b, w_down,
                  final_norm):
    f = lambda a: np.ascontiguousarray(np.asarray(a, dtype=np.float32))
    c = host_consts()
    nl = NLAYER
    an_col = f(np.asarray(attn_norm).reshape(nl, 8, 128).transpose(0, 2, 1))
    fn_col = f(np.asarray(ffn_norm).reshape(nl, 8, 128).transpose(0, 2, 1))
    gq_rep = f(np.broadcast_to(np.asarray(q_norm).reshape(1, nl * 64), (128, nl * 64)))
    gk_rep = f(np.broadcast_to(np.asarray(k_norm).reshape(1, nl * 64), (128, nl * 64)))
    gfin_rep = f(np.broadcast_to(np.asarray(final_norm).reshape(1, D), (128, D)))
    relb_rep = f(np.broadcast_to(np.asarray(rel_bias).reshape(1, 256), (128, 256)))
    cw = np.asarray(conv_w).reshape(nl, 3, 44, 128)
    cb = np.asarray(conv_b).reshape(nl, 1, 44, 128)
    cp = np.concatenate([cw, cb], axis=1)
    convp = f(cp.transpose(3, 0, 2, 1).reshape(128, nl * 44 * 4))
    return {
        "w_in": f(w_in), "w_out": f(w_out), "w_up": f(w_up), "w_down": f(w_down),
        "an_col": an_col, "fn_col": fn_col, "gq_rep": gq_rep, "gk_rep": gk_rep, "gfin_rep": gfin_rep,
        "relb_rep": relb_rep, "convp": convp, "ident": c["ident"], "cos": c["cos"], "sin": c["sin"],
        "masks": c["masks"],
    }


def kernel(x_prompt, x_sample, attn_norm, w_in, q_norm, k_norm, rel_bias, w_out, ffn_norm, w_up, conv_w, conv_b,
           w_down, final_norm):
    xp = np.asarray(x_prompt, dtype=np.float32)
    xs = np.asarray(x_sample, dtype=np.float32)
    nb_p = xp.shape[0]
    x_all = np.concatenate([xp, xs], axis=0)
    shared = shared_inputs(attn_norm, w_in, q_norm, k_norm, rel_bias, w_out, ffn_norm, w_up, conv_w, conv_b, w_down,
                           final_norm)
    nc = build()
    in_maps = []
    for cix in range(NCORES):
        m = dict(shared)
        m["x"] = np.ascontiguousarray(x_all[cix * SEQ_PER_CORE:(cix + 1) * SEQ_PER_CORE])
        in_maps.append(m)
    res = run_bass_kernel_spmd(nc, in_maps, core_ids=list(range(NCORES)))
    y_all = np.concatenate([np.asarray(r["y"], dtype=np.float32) for r in res.results], axis=0)
    return (np.ascontiguousarray(y_all[:nb_p]), np.ascontiguousarray(y_all[nb_p:]))
```

```python
import math
from contextlib import ExitStack
import numpy as np
import concourse.bass as bass
import concourse.mybir as mybir
from concourse.bass_utils import run_bass_kernel_spmd

F32 = mybir.dt.float32
BF16 = mybir.dt.bfloat16
ALU = mybir.AluOpType
AF = mybir.ActivationFunctionType
AX = mybir.AxisListType

S = 4096
D = 1024
NLAYER = 4
INW = 2304
DFF = 2816
NPAIR = 22
EPS = 1e-6
PADV = 1024
PADK = 1024
NCORES = 8
SEQ_PER_CORE = 3
BRANCHES = ((1, 0, 33), (4, 33, 9), (16, 69, 3))
NVCH = 117
SEM_ROT = 30000
DBG_HEADS = None
DBG_LEVEL = 9
MUL_POOL = False
MUL_POOL_MOD = 1
DBG_BSTAGE = 4
DBG_NBLK = None


class SemObj:
    def __init__(self):
        self.h = None
        self.cnt = 0


class Prog:
    def __init__(self):
        self.sems = []
        self.queues = []

    def newsem(self):
        s = SemObj()
        self.sems.append(s)
        return s


class Slot:
    def __init__(self, prog):
        self.prog = prog
        self.cur = prog.newsem()


class Q:
    def __init__(self, name, prog):
        self.name = name
        self.prog = prog
        self.ops = []
        self.cur = prog.newsem()
        self.seen = {}
        self.last = None
        prog.queues.append(self)

    def do(self, fn, sig=False):
        if sig:
            if self.cur.cnt >= SEM_ROT:
                self.cur = self.prog.newsem()
            self.cur.cnt += 1
            tk = (self.cur, self.cur.cnt)
            self.ops.append((0, fn, self.cur))
            self.last = tk
            return tk
        self.ops.append((0, fn, None))
        return None

    def wait(self, *tks):
        for tk in tks:
            if tk is None:
                continue
            so, v = tk
            if self.seen.get(so, 0) >= v:
                continue
            self.seen[so] = v
            self.ops.append((1, so, v))

    def dma(self, out, in_, slot):
        if slot.cur.cnt >= SEM_ROT:
            slot.cur = self.prog.newsem()
        slot.cur.cnt += 16
        self.ops.append((2, out, in_, slot.cur))
        return (slot.cur, slot.cur.cnt)

    def replay(self, e):
        for op in self.ops:
            if op[0] == 0:
                ins = op[1](e)
                if op[2] is not None:
                    ins.then_inc(op[2].h, 1)
            elif op[0] == 1:
                e.wait_ge(op[1].h, op[2])
            else:
                e.dma_start(out=op[1], in_=op[2]).then_inc(op[3].h, 16)


class Arena:
    def __init__(self, ap, nbytes):
        self.ap = ap
        self.nbytes = nbytes
        self.off = 0

    def reset(self):
        self.off = 0

    def _get(self, nbytes):
        nbytes = (nbytes + 63) // 64 * 64
        o = self.off
        self.off += nbytes
        assert self.off <= self.nbytes, ("arena overflow", self.off, self.nbytes)
        return self.ap[:, o // 2:(o + nbytes) // 2]

    def bf(self, n):
        return self._get(2 * n)[:, 0:n]

    def f32(self, n):
        return self._get(4 * n).bitcast(F32)[:, 0:n]


def t5_bucket_np(rel):
    nb = 16
    max_exact = 8
    ret = np.where(rel > 0, nb, 0)
    n = np.abs(rel)
    lg = (np.log(np.maximum(n, 1).astype(np.float32) / np.float32(max_exact))
          / np.float32(math.log(1024 / max_exact)) * np.float32(nb - max_exact))
    large = max_exact + lg.astype(np.int32)
    large = np.minimum(large, nb - 1)
    return ret + np.where(n < max_exact, n, large)


def make_masks():
    m = np.zeros((3, 32, 128, 256), np.float32)
    p = np.arange(128)[:, None]
    i = np.arange(128)[None, :]
    for br, (d, _, _) in enumerate(BRANCHES):
        for mm in range(2):
            delta = 128 * mm - 64 + p - i
            valid = np.abs(delta) <= 64
            bk = t5_bucket_np(delta * d)
            for k in range(32):
                m[br, k, :, mm * 128:(mm + 1) * 128] = ((bk == k) & valid).astype(np.float32)
    return m


def make_rope():
    t = np.arange(S)
    row = (t // 64).astype(np.float32)
    col = (t % 64).astype(np.float32)
    n = 16
    inv = (np.float32(10000.0) ** (-np.arange(n, dtype=np.float32) / np.float32(n))).astype(np.float32)
    ang = np.concatenate([row[:, None] * inv, col[:, None] * inv], axis=-1).astype(np.float32)
    return np.cos(ang).astype(np.float32), np.sin(ang).astype(np.float32)


def build(nseq=SEQ_PER_CORE, nl=NLAYER, dbg=False, stop=None):
    nc = bass.Bass("TRN2", target_bir_lowering=False)

    def din(name, shape, dtype=F32):
        return nc.dram_tensor(name, list(shape), dtype, kind="ExternalInput").ap()

    def dscr(name, shape, dtype):
        return nc.dram_tensor(name, list(shape), dtype, kind=("ExternalOutput" if dbg else "Internal")).ap()

    x_in = din("x", [nseq, S, D])
    y_out = nc.dram_tensor("y", [nseq, S, D], F32, kind="ExternalOutput").ap()
    w_in = din("w_in", [NLAYER, D, INW])
    w_out = din("w_out", [NLAYER, D, D])
    w_up = din("w_up", [NLAYER, D, 2 * DFF])
    w_down = din("w_down", [NLAYER, DFF, D])
    an_col = din("an_col", [NLAYER, 128, 8])
    fn_col = din("fn_col", [NLAYER, 128, 8])
    gq_rep = din("gq_rep", [128, NLAYER * 64])
    gk_rep = din("gk_rep", [128, NLAYER * 64])
    gfin_rep = din("gfin_rep", [128, D])
    relb_rep = din("relb_rep", [128, 256])
    convp = din("convp", [128, NLAYER * 44 * 4])
    ident_d = din("ident", [128, 128])
    cos_d = din("cos", [S, 32])
    sin_d = din("sin", [S, 32])
    masks_d = din("masks", [3, 32, 128, 256])

    xA = dscr("xA", [S, D], F32)
    xB = dscr("xB", [S, D], F32)
    qT = dscr("qT", [1024, S], BF16)
    kT = dscr("kT", [640, S], BF16)
    vd = dscr("vd", [PADV + S + PADV, 1280], BF16)
    oT = dscr("oT", [1024, S], BF16)
    winb = dscr("winb", [NLAYER, 128, 8, INW], BF16)
    woutb = dscr("woutb", [NLAYER, 128, 8, D], BF16)
    wupb = dscr("wupb", [NLAYER, NPAIR, 128, 8, 256], BF16)
    wdnb = dscr("wdnb", [NLAYER, 128, NPAIR, D], BF16)

    prog = Prog()
    pe = Q("pe", prog)
    act = Q("act", prog)
    dve = Q("dve", prog)
    pool = Q("pool", prog)
    sp = Q("sp", prog)
    queues = [pe, act, dve, pool, sp]
    pend = []

    ARENA_BYTES = 172 * 1024
    es = ExitStack()
    arena_t = es.enter_context(nc.sbuf_tensor("arena", [128, ARENA_BYTES // 2], BF16))
    identb = es.enter_context(nc.sbuf_tensor("identb", [128, 128], BF16))
    cos_t = es.enter_context(nc.sbuf_tensor("cos_t", [128, 32 * 32], F32))
    sin_t = es.enter_context(nc.sbuf_tensor("sin_t", [128, 32 * 32], F32))
    gq_t = es.enter_context(nc.sbuf_tensor("gq_t", [128, NLAYER * 64], F32))
    gk_t = es.enter_context(nc.sbuf_tensor("gk_t", [128, NLAYER * 64], F32))
    gfin_t = es.enter_context(nc.sbuf_tensor("gfin_t", [128, D], F32))
    convp_t = es.enter_context(nc.sbuf_tensor("convp_t", [128, NLAYER * 44 * 4], F32))
    ancol_t = es.enter_context(nc.sbuf_tensor("ancol_t", [128, NLAYER * 8], F32))
    fncol_t = es.enter_context(nc.sbuf_tensor("fncol_t", [128, NLAYER * 8], F32))
    E_t = es.enter_context(nc.sbuf_tensor("E_t", [128, 24 * 256], BF16))
    st_t = es.enter_context(nc.sbuf_tensor("st_t", [128, 256], F32))
    ps_t = es.enter_context(nc.psum_tensor("ps", [128, 4096], F32))

    A = Arena(arena_t[:], ARENA_BYTES)
    ident = identb[:]
    cosv = cos_t[:].rearrange("p (i c) -> p i c", c=32)
    sinv = sin_t[:].rearrange("p (i c) -> p i c", c=32)
    convv = convp_t[:].rearrange("p (l c f) -> p l c f", l=NLAYER, f=4)
    Ev = E_t[:]
    st = st_t[:]

    def bank(b, n=512):
        return ps_t[:, b * 512:b * 512 + n]

    def bank_bf(b, nb=1):
        return ps_t[:, b * 512:(b + nb) * 512].bitcast(BF16)

    def MM(out, lhsT, rhs, start, stop, sig=False):
        return pe.do(lambda e: e.matmul(out, lhsT=lhsT, rhs=rhs, start=start, stop=stop), sig)

    def TR(out, in_, sig=False):
        k = in_.shape[0]
        return pe.do(lambda e: e.transpose(out, in_, ident[0:k, 0:k]), sig)

    def ACT(out, in_, func, sig=False, **kw):
        return act.do(lambda e: e.activation(out, in_, func, **kw), sig)

    def TS(q, out, in0, s1, s2, op0, op1=None, sig=False):
        if op1 is None:
            return q.do(lambda e: e.tensor_scalar(out, in0, s1, None, op0), sig)
        return q.do(lambda e: e.tensor_scalar(out, in0, s1, s2, op0, op1), sig)

    def TT(q, out, in0, in1, op, sig=False):
        return q.do(lambda e: e.tensor_tensor(out, in0, in1, op), sig)

    def STT(q, out, in0, scalar, in1, op0, op1, sig=False):
        return q.do(lambda e: e.scalar_tensor_tensor(out, in0, scalar, in1, op0, op1), sig)

    def CP(q, out, in_, sig=False):
        return q.do(lambda e: e.tensor_copy(out, in_), sig)

    def MSET(q, ap, val, sig=False):
        return q.do(lambda e: e.memset(ap, val), sig)

    def barrier():
        tks = [q.last for q in queues if q.last is not None] + list(pend)
        for q in queues:
            q.wait(*tks)
        pend.clear()

    def rstd_chain(ssq_ap, tmp1, tmp2, out_ap, inv_n, wait_tk):
        dve.wait(wait_tk)
        t = TS(dve, tmp1, ssq_ap, inv_n, EPS, ALU.mult, ALU.add, sig=True)
        act.wait(t)
        t = ACT(tmp2, tmp1, AF.Ln, sig=True)
        act.wait(t)
        return ACT(out_ap, tmp2, AF.Exp, sig=True, scale=-0.5)

    def setup():
        A.reset()
        idf = A.f32(128)
        eb = A.f32(256)
        ebr = A.f32(256)
        mk = [A.f32(32 * 256) for _ in range(2)]
        eacc = [A.f32(256) for _ in range(2)]
        s0 = Slot(prog)
        sp.dma(idf, ident_d, s0)
        sp.dma(cosv, cos_d.rearrange("(i p) c -> p i c", p=128), s0)
        sp.dma(sinv, sin_d.rearrange("(i p) c -> p i c", p=128), s0)
        sp.dma(gq_t[:], gq_rep, s0)
        sp.dma(gk_t[:], gk_rep, s0)
        sp.dma(gfin_t[:], gfin_rep, s0)
        sp.dma(convp_t[:], convp, s0)
        sp.dma(ancol_t[:].rearrange("p (l k) -> p l k", l=NLAYER), an_col.rearrange("l p k -> p l k"), s0)
        sp.dma(fncol_t[:].rearrange("p (l k) -> p l k", l=NLAYER), fn_col.rearrange("l p k -> p l k"), s0)
        t0 = sp.dma(ebr, relb_rep, s0)
        dve.wait(t0)
        act.wait(t0)
        pool.wait(t0)
        t_id = CP(dve, ident, idf, sig=True)
        t_eb = ACT(eb, ebr, AF.Exp, sig=True)
        dve.wait(t_eb)
        pool.wait(t_eb)
        ms = [Slot(prog), Slot(prog)]
        tk_m = {}
        tk_use = {}
        for br in range(3):
            sp.wait(tk_use.get(br - 2))
            tk_m[br] = sp.dma(mk[br % 2].rearrange("p (k c) -> p k c", k=32),
                              masks_d[br].rearrange("k p c -> p k c"), ms[br % 2])
            last = []
            for h in range(8):
                q = dve
                q.wait(tk_m[br])
                ea = eacc[h % 2]
                mv = mk[br % 2].rearrange("p (k c) -> p k c", k=32)
                for k in range(32):
                    col = eb[:, k * 8 + h:k * 8 + h + 1]
                    if k == 0:
                        TS(q, ea, mv[:, k, :], col, None, ALU.mult)
                    else:
                        STT(q, ea, mv[:, k, :], col, ea, ALU.mult, ALU.add)
                t = CP(q, Ev[:, (h * 3 + br) * 256:(h * 3 + br + 1) * 256], ea, sig=True)
                if h >= 6:
                    last.append(t)
            sp.wait(*last)
            tk_use[br] = last[-1]
        z = A.bf(1280)
        t = MSET(pool, z, 0.0, sig=True)
        pool.wait(t)
        zs = Slot(prog)
        for j in range(PADV // 128):
            pend.append(pool.dma(vd[j * 128:(j + 1) * 128, :], z, zs))
            pend.append(pool.dma(vd[PADV + S + j * 128:PADV + S + (j + 1) * 128, :], z, zs))
        barrier()

    def prepass():
        A.reset()
        stg = [A.f32(5632) for _ in range(2)]
        ob = [A.bf(5632) for _ in range(2)]
        ls = [Slot(prog), Slot(prog)]
        ss = [Slot(prog), Slot(prog)]
        items = []
        for l in range(nl):
            for k in range(8):
                items.append((w_in[l, k * 128:(k + 1) * 128, :], INW, ancol_t[:, l * 8 + k:l * 8 + k + 1],
                              lambda o, l=l, k=k: [(winb[l, :, k, :], o)]))
            for k in range(8):
                items.append((w_out[l, k * 128:(k + 1) * 128, :], D, None,
                              lambda o, l=l, k=k: [(woutb[l, :, k, :], o)]))
            for k in range(8):
                items.append((w_up[l, k * 128:(k + 1) * 128, :], 2 * DFF, fncol_t[:, l * 8 + k:l * 8 + k + 1],
                              lambda o, l=l, k=k: [
                                  (wupb[l].rearrange("c p k (g j) -> p k g c j", g=2)[:, k, g],
                                   o[:, g * DFF:(g + 1) * DFF].rearrange("p (c j) -> p c j", j=128))
                                  for g in range(2)]))
            for c in range(NPAIR):
                items.append((w_down[l, c * 128:(c + 1) * 128, :], D, None,
                              lambda o, l=l, c=c: [(wdnb[l, :, c, :], o)]))
        tk_ld = {}
        tk_cv = {}
        tk_st = {}

        def load(i):
            src, n, _, _ = items[i]
            sp.wait(tk_cv.get(i - 2))
            tk_ld[i] = sp.dma(stg[i % 2][:, 0:n], src, ls[i % 2])

        load(0)
        for i in range(len(items)):
            if i + 1 < len(items):
                load(i + 1)
            src, n, sc, dstf = items[i]
            q = dve if i % 2 == 0 else act
            q.wait(tk_ld[i], tk_st.get(i - 2))
            if q is dve:
                if sc is None:
                    tk_cv[i] = CP(dve, ob[i % 2][:, 0:n], stg[i % 2][:, 0:n], sig=True)
                else:
                    tk_cv[i] = TS(dve, ob[i % 2][:, 0:n], stg[i % 2][:, 0:n], sc, None, ALU.mult, sig=True)
            else:
                if sc is None:
                    tk_cv[i] = ACT(ob[i % 2][:, 0:n], stg[i % 2][:, 0:n], AF.Copy, sig=True)
                else:
                    tk_cv[i] = ACT(ob[i % 2][:, 0:n], stg[i % 2][:, 0:n], AF.Copy, sig=True, scale=sc)
            pool.wait(tk_cv[i])
            for d_ap, s_ap in dstf(ob[i % 2][:, 0:n]):
                tk_st[i] = pool.dma(d_ap, s_ap, ss[i % 2])
        pend.extend(tk_st.values())
        barrier()

    slotW = Slot(prog)
    _slots = {}

    def SL(name):
        if name not in _slots:
            _slots[name] = Slot(prog)
        return _slots[name]

    def phaseP(s, l):
        A.reset()
        xt = [A.f32(1024) for _ in range(2)]
        junk = A.bf(1024)
        hb = A.bf(1024)
        hT = A.bf(1024)
        win = A.bf(8 * INW)
        tq = A.f32(640)
        sq = A.f32(640)
        r1 = A.f32(320)
        r2 = A.f32(320)
        G = A.f32(640)
        qk = [A.bf(1664) for _ in range(2)]
        vt = [A.bf(1280) for _ in range(2)]
        stg = [A.bf(13 * 512) for _ in range(2)]
        barrier()
        xsrc = x_in[s] if l == 0 else xA
        winv = win.rearrange("p (k n) -> p k n", k=8)
        tw = sp.dma(winv, winb[l], slotW)
        Gv = G.rearrange("p (h c) -> p h c", h=10)
        CP(pool, Gv[:, 0:8, :], gq_t[:, l * 64:(l + 1) * 64].unsqueeze(1).to_broadcast([128, 8, 64]))
        CP(pool, Gv[:, 8:10, :], gk_t[:, l * 64:(l + 1) * 64].unsqueeze(1).to_broadcast([128, 2, 64]))
        for b in range(2):
            tG = MSET(pool, vt[b].rearrange("p (h e) -> p h e", h=10)[:, :, 64:128], 1.0, sig=True)
        dve.wait(tG)
        psT = bank_bf(0)
        psQ = bank_bf(6, 2)
        groups = ((0, 512, 1), (512, 768, 2), (768, 1280, 3), (1280, 1792, 4), (1792, 2304, 5))
        sx = [SL("P.sx0"), SL("P.sx1")]
        sv = [SL("P.sv0"), SL("P.sv1")]
        sg = [SL("P.sg0"), SL("P.sg1")]
        tk_x, tk_sq, tk_hb, tk_tr, tk_hT, tk_proj = {}, {}, {}, {}, {}, {}
        tk_eva, tk_evd, tk_rope, tk_tr2, tk_stg, tk_sd, tk_vst = {}, {}, {}, {}, {}, {}, {}

        def load(i):
            sp.wait(tk_hb.get(i - 2), tk_sq.get(i - 2))
            tk_x[i] = sp.dma(xt[i % 2], xsrc[i * 128:(i + 1) * 128, :], sx[i % 2])

        def c(i, j, n=1):
            c0 = (i % 4) * 48
            return st[:, c0 + j:c0 + j + n]

        t_tq, t_qb = {}, {}

        def stageN(i):
            sl = i % 2
            act.wait(tk_x[i])
            tk_sq[i] = ACT(junk, xt[sl], AF.Square, sig=True, accum_out=c(i, 0))
            t_rs = rstd_chain(c(i, 0), c(i, 1), c(i, 2), c(i, 3), 1.0 / D, tk_sq[i])
            dve.wait(t_rs, tk_tr.get(i - 1))
            tk_hb[i] = TS(dve, hb, xt[sl], c(i, 3), None, ALU.mult, sig=True)

        def stageT(i):
            pe.wait(tk_hb[i], tk_hT.get(i - 1))
            for k in range(8):
                t = TR(psT[:, k * 128:(k + 1) * 128], hb[:, k * 128:(k + 1) * 128], sig=(k == 7))
            tk_tr[i] = t
            dve.wait(tk_tr[i], tk_proj.get(i - 1))
            tk_hT[i] = CP(dve, hT, psT, sig=True)
            pe.wait(tk_hT[i], tw, tk_eva.get(i - 1), tk_evd.get(i - 1))
            for k in range(8):
                for gi, (a0, a1, bk) in enumerate(groups):
                    t = MM(bank(bk, a1 - a0), hT[:, k * 128:(k + 1) * 128], winv[:, k, a0:a1], k == 0, k == 7,
                           sig=(k == 7 and gi == 4))
            tk_proj[i] = t

        def stageE1(i):
            sl = i % 2
            qkc = qk[i % 2]
            act.wait(tk_proj[i], tk_rope.get(i - 1))
            ACT(tq[:, 0:512], bank(1), AF.Copy)
            t_tq[i] = ACT(tq[:, 512:640], bank(2, 128), AF.Copy, sig=True)
            act.wait(tk_tr2.get(i - 2))
            ACT(qkc[:, 640:1152], bank(3), AF.Copy)
            t_qb[i] = ACT(qkc[:, 1152:1664], bank(4), AF.Copy, sig=True)
            tk_eva[i] = t_qb[i]
            dve.wait(tk_proj[i], tk_vst.get(i - 2))
            vt3 = vt[sl].rearrange("p (h e) -> p h e", h=10)
            CP(dve, vt3[:, 0:2, 0:64], bank(2, 256)[:, 128:256].rearrange("p (h e) -> p h e", h=2))
            tk_evd[i] = CP(dve, vt3[:, 2:10, 0:64], bank(5).rearrange("p (h e) -> p h e", h=8), sig=True)
            pool.wait(tk_evd[i])
            tk_vst[i] = pool.dma(vd[PADV + i * 128:PADV + (i + 1) * 128, :], vt[sl], sv[sl])
            pend.append(tk_vst[i])

        def stageE2(i):
            qkc = qk[i % 2]
            dve.wait(t_tq[i])
            tq3 = tq.rearrange("p (h c) -> p h c", h=10)
            TT(dve, sq, tq, tq, ALU.mult)
            t_hs = dve.do(lambda e, o=c(i, 4, 10), i_=sq.rearrange("p (h c) -> p h c", h=10): e.tensor_reduce(
                o, i_, AX.X, ALU.add), sig=True)
            t_rsh = rstd_chain(c(i, 4, 10), c(i, 14, 10), c(i, 24, 10), c(i, 34, 10), 1.0 / 64, t_hs)
            dve.wait(t_rsh, tk_tr2.get(i - 2))
            TT(dve, tq3, tq3, c(i, 34, 10).unsqueeze(2).to_broadcast([128, 10, 64]), ALU.mult)
            TT(dve, tq3, tq3, Gv, ALU.mult)
            x0 = tq3[:, :, 0:64:2]
            x1 = tq3[:, :, 1:64:2]
            cb = cosv[:, i, :].unsqueeze(1).to_broadcast([128, 10, 32])
            sb = sinv[:, i, :].unsqueeze(1).to_broadcast([128, 10, 32])
            r1v = r1.rearrange("p (h c) -> p h c", h=10)
            r2v = r2.rearrange("p (h c) -> p h c", h=10)
            qk3 = qkc[:, 0:640].rearrange("p (h c) -> p h c", h=10)
            TT(dve, r1v, x0, cb, ALU.mult)
            TT(dve, r2v, x1, sb, ALU.mult)
            TT(dve, qk3[:, :, 0:64:2], r1v, r2v, ALU.subtract)
            TT(dve, r1v, x0, sb, ALU.mult)
            TT(dve, r2v, x1, cb, ALU.mult)
            tk_rope[i] = TT(dve, qk3[:, :, 1:64:2], r1v, r2v, ALU.add, sig=True)

        def stageT2(i):
            qkc = qk[i % 2]
            pe.wait(tk_rope[i], t_qb[i], tk_stg.get(i - 1))
            for j in range(13):
                t = TR(psQ[:, j * 128:(j + 1) * 128], qkc[:, j * 128:(j + 1) * 128], sig=(j == 12))
            tk_tr2[i] = t
            g = i // 4
            dve.wait(tk_tr2[i], tk_sd.get(g - 2))
            sg3 = stg[g % 2].rearrange("p (j t) -> p j t", j=13)
            tk_stg[i] = CP(dve, sg3[:, :, (i % 4) * 128:(i % 4 + 1) * 128],
                           psQ[:, 0:1664].rearrange("p (j t) -> p j t", j=13), sig=True)
            if i % 4 == 3:
                pool.wait(tk_stg[i])
                t0 = g * 512
                pool.dma(qT[0:512, t0:t0 + 512].rearrange("(j p) t -> p j t", p=128), sg3[:, 0:4, :], sg[g % 2])
                pool.dma(kT[0:128, t0:t0 + 512], sg3[:, 4, :], sg[g % 2])
                pool.dma(qT[512:1024, t0:t0 + 512].rearrange("(j p) t -> p j t", p=128), sg3[:, 5:9, :], sg[g % 2])
                tk_sd[g] = pool.dma(kT[128:640, t0:t0 + 512].rearrange("(j p) t -> p j t", p=128), sg3[:, 9:13, :],
                                    sg[g % 2])
                pend.append(tk_sd[g])

        load(0)
        load(1)
        stageN(0)
        stageT(0)
        for i in range(32):
            if i + 2 < 32:
                load(i + 2)
            stageE1(i)
            if i + 1 < 32:
                stageN(i + 1)
                stageT(i + 1)
            stageE2(i)
            stageT2(i)

    def phaseATT(s, l):
        A.reset()
        qh = [A.bf(S) for _ in range(2)]
        kh = [A.bf(PADK + S + PADK) for _ in range(2)]
        vh = [A.bf(NVCH * 128) for _ in range(2)]
        pT = [A.bf(1024) for _ in range(3)]
        NSB = 4
        pex = [A.f32(512) for _ in range(NSB)]
        pTb = [A.bf(512) for _ in range(NSB)]
        acc = A.f32(S)
        oh = [A.bf(S) for _ in range(2)]
        oh2 = [A.bf(S) for _ in range(2)]
        rd = A.f32(512)
        barrier()
        for b in range(2):
            MSET(pool, kh[b][:, 0:PADK], 0.0)
            tkz = MSET(pool, kh[b][:, PADK + S:PADK + S + PADK], 0.0, sig=True)
        sp.wait(tkz)
        pe.wait(tkz)
        sh = [SL("A.sh0"), SL("A.sh1")]
        so = [SL("A.so0"), SL("A.so1")]
        heads = [("A", i) for i in range(4)] + [("B", h) for h in range(8)]
        if DBG_HEADS is not None:
            heads = DBG_HEADS
        tk_head, tk_done, tk_ohst = {}, {}, {}
        psS2 = [ps_t[:, 0:1024], ps_t[:, 1024:2048]]
        psOa = [bank(4), bank(6)]
        psOb = [bank(5), bank(7)]
        psSB = [bank(0), bank(1), bank(2), bank(3)]
        psOB = [bank(4, 256), bank(5, 256), bank(6, 256), bank(7, 256)]
        cnt = {"c": 0, "q": 0, "b": 0}
        tk_qk, tk_exp, tk_pv, tk_norm = {}, {}, {}, {}
        tk_s1, tk_eB, tk_mB, tk_pvB, tk_accB = {}, {}, {}, {}, {}

        def loadhead(n):
            typ, h = heads[n]
            b = n % 2
            sp.wait(tk_done.get(n - 2))
            if typ == "A":
                g = h // 2
                sp.dma(qh[b][:, :], qT[h * 128:(h + 1) * 128, :], sh[b])
                sp.dma(kh[b][0:64, PADK:PADK + S], kT[g * 64:(g + 1) * 64, :], sh[b])
                sp.dma(kh[b][64:128, PADK:PADK + S], kT[g * 64:(g + 1) * 64, :], sh[b])
                tk = sp.dma(vh[b][:, 0:32 * 128].rearrange("p (m e) -> p m e", e=128),
                            vd[PADV:PADV + S, g * 128:(g + 1) * 128].rearrange("(m p) e -> p m e", p=128), sh[b])
            else:
                sp.dma(qh[b][0:64, :], qT[512 + h * 64:512 + (h + 1) * 64, :], sh[b])
                sp.dma(kh[b][0:64, PADK:PADK + S], kT[128 + h * 64:128 + (h + 1) * 64, :], sh[b])
                for (d, base, nm) in BRANCHES:
                    for r in range(d):
                        r0 = PADV - 64 * d + r
                        src = vd[r0:r0 + (128 * nm - 1) * d + 1:d, (2 + h) * 128:(3 + h) * 128]
                        tk = sp.dma(vh[b][:, (base + r * nm) * 128:(base + (r + 1) * nm) * 128].rearrange(
                            "p (m e) -> p m e", e=128), src.rearrange("(m p) e -> p m e", p=128), sh[b])
            tk_head[n] = tk

        def normalize(src_num, src_den, dst, sig):
            dve.do(lambda e: e.reciprocal(rd[64:128, :], src_den))
            CP(dve, rd[0:64, :], rd[64:128, :])
            return TT(dve, dst, src_num, rd[0:64, :], ALU.mult, sig=sig)

        def headA(n, i):
            b = n % 2
            jobs = [(qt, c) for qt in range(8) for c in range(32)]
            base = cnt["c"]
            qbase = cnt["q"]

            def QK(j):
                qt, c = jobs[j]
                gi = base + j
                sl = gi % 2
                pe.wait(tk_head[n], tk_exp.get(gi - 2))
                if j == 0:
                    pe.wait(dve.last)
                MM(psS2[sl][:, 0:512], kh[b][0:64, PADK + c * 128:PADK + (c + 1) * 128],
                   qh[b][0:64, qt * 512:(qt + 1) * 512], True, True)
                tk_qk[gi] = MM(psS2[sl][:, 512:1024], kh[b][64:128, PADK + c * 128:PADK + (c + 1) * 128],
                               qh[b][64:128, qt * 512:(qt + 1) * 512], True, True, sig=True)

            QK(0)
            for j, (qt, c) in enumerate(jobs):
                gi = base + j
                gq = qbase + qt
                if j + 1 < len(jobs):
                    QK(j + 1)
                act.wait(tk_qk[gi], tk_pv.get(gi - 3))
                tk_exp[gi] = ACT(pT[gi % 3], psS2[gi % 2], AF.Exp, sig=True, scale=0.125)
                pe.wait(tk_exp[gi])
                if c == 0:
                    pe.wait(tk_norm.get(gq - 2))
                MM(psOa[gq % 2], vh[b][:, c * 128:(c + 1) * 128], pT[gi % 3][:, 0:512], c == 0, c == 31)
                tk_pv[gi] = MM(psOb[gq % 2], vh[b][:, c * 128:(c + 1) * 128], pT[gi % 3][:, 512:1024], c == 0,
                               c == 31, sig=True)
                if c == 31:
                    dve.wait(tk_pv[gi], tk_ohst.get(n - 2))
                    normalize(psOa[gq % 2][0:64, :], psOa[gq % 2][64:128, :],
                              oh[b][0:64, qt * 512:(qt + 1) * 512], False)
                    tk_norm[gq] = normalize(psOb[gq % 2][0:64, :], psOb[gq % 2][64:128, :],
                                            oh2[b][0:64, qt * 512:(qt + 1) * 512], True)
            cnt["c"] += len(jobs)
            cnt["q"] += 8
            tk_done[n] = tk_pv[base + len(jobs) - 1]
            pool.wait(tk_norm[qbase + 7])
            pool.dma(oT[i * 128:i * 128 + 64, :], oh[b][0:64, :], so[b])
            tk_ohst[n] = pool.dma(oT[i * 128 + 64:i * 128 + 128, :], oh2[b][0:64, :], so[b])
            pend.append(tk_ohst[n])

        def headB(n, h):
            b = n % 2
            items = []
            for br, (d, vbase, nm) in enumerate(BRANCHES):
                if DBG_LEVEL < 7 and br != DBG_LEVEL:
                    continue
                for r in range(d):
                    for bp in range(16 // d):
                        items.append((br, d, vbase, nm, r, bp))
            if DBG_LEVEL == 7:
                items = items[:1]
            base = cnt["b"]
            if n == 0 or heads[n - 1][0] == "A":
                pe.wait(act.last, dve.last)

            def S1(j):
                br, d, vbase, nm, r, bp = items[j]
                gi = base + j
                pe.wait(tk_head[n], tk_eB.get(gi - NSB))
                bk = 2 * bp
                q0 = r + d * 128 * bk
                kc = [PADK + r + d * (128 * (bk + m) - 64) for m in range(3)]
                MM(psSB[gi % NSB][:, 0:128], kh[b][0:64, kc[0]:kc[0] + 127 * d + 1:d],
                   qh[b][0:64, q0:q0 + 127 * d + 1:d], True, True)
                MM(psSB[gi % NSB][:, 128:384], kh[b][0:64, kc[1]:kc[1] + 127 * d + 1:d],
                   qh[b][0:64, q0:q0 + 255 * d + 1:d], True, True)
                t = MM(psSB[gi % NSB][:, 384:512], kh[b][0:64, kc[2]:kc[2] + 127 * d + 1:d],
                       qh[b][0:64, q0 + 128 * d:q0 + 255 * d + 1:d], True, True, sig=True)
                tk_s1[gi] = t

            def EXPMUL(j):
                br, d, vbase, nm, r, bp = items[j]
                gi = base + j
                act.wait(tk_s1[gi], tk_mB.get(gi - NSB))
                tk_eB[gi] = ACT(pex[gi % NSB], psSB[gi % NSB], AF.Exp, sig=True, scale=0.125)
                mq = pool if (MUL_POOL and gi % MUL_POOL_MOD == 0) else dve
                mq.wait(tk_eB[gi], tk_pvB.get(gi - NSB))
                Eb = Ev[:, (h * 3 + br) * 256:(h * 3 + br + 1) * 256].unsqueeze(1).to_broadcast([128, 2, 256])
                tk_mB[gi] = TT(mq, pTb[gi % NSB].rearrange("p (a c) -> p a c", a=2),
                               pex[gi % NSB].rearrange("p (a c) -> p a c", a=2), Eb, ALU.mult, sig=True)

            for jj in range(min(3, len(items))):
                S1(jj)
            EXPMUL(0)
            if len(items) > 1:
                EXPMUL(1)
            for j, (br, d, vbase, nm, r, bp) in enumerate(items):
                gi = base + j
                if j + 3 < len(items):
                    S1(j + 3)
                if j + 2 < len(items):
                    EXPMUL(j + 2)
                pe.wait(tk_mB[gi], tk_accB.get(gi - NSB))
                ch = vbase + r * nm + 2 * bp
                MM(psOB[gi % NSB][:, 0:256], vh[b][:, (ch + 1) * 128:(ch + 2) * 128], pTb[gi % NSB][:, 128:384], True, False)
                MM(psOB[gi % NSB][:, 0:128], vh[b][:, ch * 128:(ch + 1) * 128], pTb[gi % NSB][:, 0:128], False, True)
                t = MM(psOB[gi % NSB][:, 128:256], vh[b][:, (ch + 2) * 128:(ch + 3) * 128], pTb[gi % NSB][:, 384:512],
                       False, True, sig=True)
                tk_pvB[gi] = t
                dve.wait(t)
                q0 = r + d * 256 * bp
                accv = acc[:, q0:q0 + 255 * d + 1:d]
                if br == 0:
                    tk_accB[gi] = CP(dve, accv, psOB[gi % NSB], sig=True)
                else:
                    tk_accB[gi] = TT(dve, accv, accv, psOB[gi % NSB], ALU.add, sig=True)
            cnt["b"] += len(items)
            tk_done[n] = tk_pvB[base + len(items) - 1]
            dve.wait(tk_ohst.get(n - 2))
            for qt in range(8):
                t = normalize(acc[0:64, qt * 512:(qt + 1) * 512], acc[64:128, qt * 512:(qt + 1) * 512],
                              oh[b][0:64, qt * 512:(qt + 1) * 512], qt == 7)
            pool.wait(t)
            tk_ohst[n] = pool.dma(oT[512 + h * 64:512 + (h + 1) * 64, :], oh[b][0:64, :], so[b])
            pend.append(tk_ohst[n])

        loadhead(0)
        for n, (typ, h) in enumerate(heads):
            if n + 1 < len(heads):
                loadhead(n + 1)
            if typ == "A":
                headA(n, h)
            else:
                headB(n, h)

    def phaseO(s, l):
        A.reset()
        wout = A.bf(8 * D)
        ot = [A.bf(8 * 512) for _ in range(2)]
        xt = [A.f32(D) for _ in range(3)]
        barrier()
        woutv = wout.rearrange("p (k n) -> p k n", k=8)
        tw = sp.dma(woutv, woutb[l], slotW)
        xsrc = x_in[s] if l == 0 else xA
        sot = [SL("O.sot0"), SL("O.sot1")]
        sx = [SL("O.sx%d" % i) for i in range(3)]
        sxs = [SL("O.sxs%d" % i) for i in range(3)]
        psX = [ps_t[:, 0:1024], ps_t[:, 1024:2048]]
        tk_ot, tk_x, tk_mm, tk_add, tk_xst = {}, {}, {}, {}, {}
        for g in range(8):
            sp.wait(tk_mm.get((g - 2) * 4 + 3))
            otv = ot[g % 2].rearrange("p (k t) -> p k t", k=8)
            tk_ot[g] = sp.dma(otv, oT[:, g * 512:(g + 1) * 512].rearrange("(k p) t -> p k t", p=128), sot[g % 2])
            for j in range(4):
                i = g * 4 + j
                sp.wait(tk_xst.get(i - 3))
                tk_x[i] = sp.dma(xt[i % 3], xsrc[i * 128:(i + 1) * 128, :], sx[i % 3])
                pe.wait(tk_ot[g], tw, tk_add.get(i - 2))
                for n2 in range(2):
                    for k in range(8):
                        t = MM(psX[i % 2][:, n2 * 512:(n2 + 1) * 512], otv[:, k, j * 128:(j + 1) * 128],
                               woutv[:, k, n2 * 512:(n2 + 1) * 512], k == 0, k == 7, sig=(n2 == 1 and k == 7))
                tk_mm[i] = t
                dve.wait(t, tk_x[i])
                TT(dve, xt[i % 3][:, 0:512], xt[i % 3][:, 0:512], psX[i % 2][:, 0:512], ALU.add)
                tk_add[i] = TT(dve, xt[i % 3][:, 512:1024], xt[i % 3][:, 512:1024], psX[i % 2][:, 512:1024], ALU.add,
                               sig=True)
                pool.wait(tk_add[i])
                tk_xst[i] = pool.dma(xB[i * 128:(i + 1) * 128, :], xt[i % 3], sxs[i % 3])
                pend.append(tk_xst[i])

    def phaseF(s, l, last):
        A.reset()
        xt5 = [A.f32(5 * D) for _ in range(2)]
        junk = A.bf(D)
        hb = A.bf(5 * D)
        h2T = A.bf(8 * 520)
        wup = [A.bf(8 * 256) for _ in range(3)]
        wdn = A.bf(NPAIR * D)
        U = [[A.f32(516) for _ in range(2)] for _ in range(2)]
        cg = [A.f32(512) for _ in range(2)]
        sgl = [A.f32(512) for _ in range(2)]
        cv = A.f32(512)
        aT = A.bf(NPAIR * 512)
        barrier()
        wdnv = wdn.rearrange("p (c n) -> p c n", c=NPAIR)
        twd = sp.dma(wdnv, wdnb[l], slotW)
        h2Tv = h2T.rearrange("p (k t) -> p k t", k=8)
        aTv = aT.rearrange("p (c t) -> p c t", c=NPAIR)
        hb3 = hb.rearrange("p (j d) -> p j d", j=5)
        sxl = [SL("F.sxl0"), SL("F.sxl1")]
        sws = [SL("F.sws%d" % i) for i in range(3)]
        sxs = [SL("F.sxs0"), SL("F.sxs1")]
        psT = [bank_bf(0), bank_bf(1)]
        psHs = [bank(0, 16), bank(1, 16)]
        psGV = [(bank(2), bank(3)), (bank(4), bank(5))]
        psD = [ps_t[:, 3072:4096], ps_t[:, 1024:2048]]
        dst = y_out[s] if last else xA
        tk_x, tk_st, tk_w, tk_pair, tk_ev, tk_a, tk_dn, tk_h2 = {}, {}, {}, {}, {}, {}, {}, {}
        gpc = {"p": 0, "d": 0, "t": 0}

        def loadx(T):
            slx = T % 2
            t0 = T * 512
            x5 = xt5[slx].rearrange("p (j d) -> p j d", j=5)
            pool.wait(*tk_st.get(T - 2, ()))
            tz = MSET(pool, x5[0:2, 4, :], 0.0, sig=True)
            sp.wait(tz)
            sp.wait(*tk_st.get(T - 2, ()))
            tk = sp.dma(x5[:, 0:4, :], xB[t0:t0 + 512, :].rearrange("(j p) d -> p j d", p=128), sxl[slx])
            if T > 0:
                tk = sp.dma(x5[0:1, 4, :], xB[t0 - 1:t0, :], sxl[slx])
            if T < 7:
                tk = sp.dma(x5[1:2, 4, :], xB[t0 + 512:t0 + 513, :], sxl[slx])
            tk_x[T] = tk

        def loadw(T, c):
            gi = T * NPAIR + c
            sp.wait(tk_pair.get(gi - 3))
            tk_w[gi] = sp.dma(wup[gi % 3].rearrange("p (k n) -> p k n", k=8), wupb[l, c], sws[gi % 3])

        loadx(0)
        for T in range(8):
            if T + 1 < 8:
                loadx(T + 1)
            slx = T % 2
            x5 = xt5[slx].rearrange("p (j d) -> p j d", j=5)
            c0 = (T % 2) * 64

            def c(j, n=1):
                return st[:, c0 + j:c0 + j + n]
            loadw(T, 0)
            loadw(T, 1)
            act.wait(tk_x[T])
            for j in range(5):
                npp = 128 if j < 4 else 2
                t = ACT(junk[0:npp, :], x5[0:npp, j, :], AF.Square, sig=(j == 4), accum_out=st[0:npp, c0 + j:c0 + j + 1])
            t_rs = rstd_chain(c(0, 5), c(8, 5), c(16, 5), c(24, 5), 1.0 / D, t)
            dve.wait(t_rs, tk_h2.get(T - 1))
            for j in range(5):
                npp = 128 if j < 4 else 2
                t = TS(dve, hb3[0:npp, j, :], x5[0:npp, j, :], st[0:npp, c0 + 24 + j:c0 + 25 + j], None, ALU.mult,
                       sig=True)
            t_hb = t
            pe.wait(t_hb, tk_dn.get(T - 1), tk_dn.get(("d", gpc["d"] - 1)), tk_dn.get(("d", gpc["d"] - 2)),
                    *tk_ev.get(T - 1, ()))
            tcs = []
            for j in range(5):
                gt = gpc["t"]
                gpc["t"] += 1
                npp = 128 if j < 4 else 2
                pe.wait(tk_h2.get(("c", gt - 2)))
                for k in range(8):
                    t = TR(psT[gt % 2][:, k * 128:k * 128 + npp], hb3[0:npp, j, k * 128:(k + 1) * 128], sig=(k == 7))
                dve.wait(t)
                if j < 4:
                    t = CP(dve, h2Tv[:, :, j * 128:(j + 1) * 128],
                           psT[gt % 2].rearrange("p (k t) -> p k t", k=8), sig=True)
                else:
                    t = CP(dve, h2Tv[:, :, 512:514],
                           psT[gt % 2].rearrange("p (k t) -> p k t", k=8)[:, :, 0:2], sig=True)
                tk_h2[("c", gt)] = t
            t_h2 = t
            tk_ev[T] = []
            for cpi in range(NPAIR):
                gi = T * NPAIR + cpi
                if cpi + 2 < NPAIR:
                    loadw(T, cpi + 2)
                par = gi % 2
                wv = wup[gi % 3].rearrange("p (k n) -> p k n", k=8)
                pG, pV = psGV[par]
                hoff = 0
                psH = psHs[par]
                pe.wait(tk_w[gi], t_h2, tk_a.get(gi - 2))
                for (pp, n0, ho) in ((pG, 0, hoff), (pV, 128, hoff + 2)):
                    for k in range(8):
                        MM(pp, wv[:, k, n0:n0 + 128], h2Tv[:, k, 0:512], k == 0, k == 7)
                        t = MM(psH[:, ho:ho + 2], wv[:, k, n0:n0 + 128], h2Tv[:, k, 512:514], k == 0, k == 7,
                               sig=(k == 7 and n0 == 128))
                tk_pair[gi] = t
                Ug, Uv = U[par]
                act.wait(t, tk_a.get(gi - 2))
                ACT(Ug[:, 1:513], pG, AF.Copy)
                ACT(Ug[:, 0:514:513], psH[:, hoff:hoff + 2], AF.Copy)
                ACT(Uv[:, 1:513], pV, AF.Copy)
                t_ev = ACT(Uv[:, 0:514:513], psH[:, hoff + 2:hoff + 4], AF.Copy, sig=True)
                tk_ev[T] = [t_ev]
                dve.wait(t_ev)
                cgp = cg[par]
                for (Ux, co, dstc) in ((Ug, cpi, cgp), (Uv, NPAIR + cpi, cv)):
                    TS(dve, dstc, Ux[:, 1:513], convv[:, l, co, 1:2], convv[:, l, co, 3:4], ALU.mult, ALU.add)
                    STT(dve, dstc, Ux[:, 0:512], convv[:, l, co, 0:1], dstc, ALU.mult, ALU.add)
                    t = STT(dve, dstc, Ux[:, 2:514], convv[:, l, co, 2:3], dstc, ALU.mult, ALU.add, sig=(Ux is Ug))
                    if Ux is Ug:
                        t_cg = t
                act.wait(t_cg)
                t_sg = ACT(sgl[par], cgp, AF.Silu, sig=True)
                dve.wait(t_sg)
                if cpi == 0:
                    dve.wait(tk_dn.get(T - 1))
                tk_a[gi] = TT(dve, aTv[:, cpi, :], sgl[par], cv, ALU.mult, sig=True)
            tk_st[T] = []
            for s4 in range(4):
                gd = gpc["d"]
                gpc["d"] += 1
                pD = psD[gd % 2]
                pe.wait(tk_a[T * NPAIR + NPAIR - 1], twd, tk_dn.get(("d", gd - 2)))
                for n2 in range(2):
                    for cc in range(NPAIR):
                        t = MM(pD[:, n2 * 512:(n2 + 1) * 512], aTv[:, cc, s4 * 128:(s4 + 1) * 128],
                               wdnv[:, cc, n2 * 512:(n2 + 1) * 512], cc == 0, cc == NPAIR - 1,
                               sig=(n2 == 1 and cc == NPAIR - 1))
                tk_dn[T] = t
                dve.wait(t)
                xs = x5[:, s4, :]
                TT(dve, xs[:, 0:512], xs[:, 0:512], pD[:, 0:512], ALU.add)
                t = TT(dve, xs[:, 512:1024], xs[:, 512:1024], pD[:, 512:1024], ALU.add, sig=True)
                tk_dn[("d", gd)] = t
                if last:
                    cc0 = c0 + 32 + s4 * 4
                    act.wait(t)
                    t = ACT(junk, xs, AF.Square, sig=True, accum_out=st[:, cc0:cc0 + 1])
                    t = rstd_chain(st[:, cc0:cc0 + 1], st[:, cc0 + 1:cc0 + 2], st[:, cc0 + 2:cc0 + 3],
                                   st[:, cc0 + 3:cc0 + 4], 1.0 / D, t)
                    dve.wait(t)
                    t = STT(dve, xs, xs, st[:, cc0 + 3:cc0 + 4], gfin_t[:], ALU.mult, ALU.mult, sig=True)
                pool.wait(t)
                r0 = T * 512 + s4 * 128
                tks = pool.dma(dst[r0:r0 + 128, :], xs, sxs[slx])
                tk_st[T].append(tks)
                pend.append(tks)

    setup()
    if stop != "setup":
        prepass()
    for s in range(nseq):
        for l in range(nl):
            if stop in ("setup", "prepass"):
                break
            phaseP(s, l)
            if stop == "P":
                break
            phaseATT(s, l)
            if stop == "ATT":
                break
            phaseO(s, l)
            if stop == "O":
                break
            phaseF(s, l, l == nl - 1)
    barrier()

    for so in prog.sems:
        so.h = es.enter_context(nc.semaphore())
    with nc.Block() as block:
        @block.tensor
        def _(e):
            pe.replay(e)

        @block.scalar
        def _(e):
            act.replay(e)

        @block.vector
        def _(e):
            dve.replay(e)

        @block.gpsimd
        def _(e):
            pool.replay(e)

        @block.sync
        def _(e):
            sp.replay(e)
    es.close()
    return nc


_CONST = {}


def host_consts():
    if not _CONST:
        cos, sin = make_rope()
        _CONST["cos"] = cos
        _CONST["sin"] = sin
        _CONST["masks"] = make_masks()
        _CONST["ident"] = np.eye(128, dtype=np.float32)
    return _CONST


def shared_inputs(attn_norm, w_in, q_norm, k_norm, rel_bias, w_out, ffn_norm, w_up, conv_w, conv_b, w_down,
                  final_norm):
    f = lambda a: np.ascontiguousarray(np.asarray(a, dtype=np.float32))
    c = host_consts()
    nl = NLAYER
    an_col = f(np.asarray(attn_norm).reshape(nl, 8, 128).transpose(0, 2, 1))
    fn_col = f(np.asarray(ffn_norm).reshape(nl, 8, 128).transpose(0, 2, 1))
    gq_rep = f(np.broadcast_to(np.asarray(q_norm).reshape(1, nl * 64), (128, nl * 64)))
    gk_rep = f(np.broadcast_to(np.asarray(k_norm).reshape(1, nl * 64), (128, nl * 64)))
    gfin_rep = f(np.broadcast_to(np.asarray(final_norm).reshape(1, D), (128, D)))
    relb_rep = f(np.broadcast_to(np.asarray(rel_bias).reshape(1, 256), (128, 256)))
    cw = np.asarray(conv_w).reshape(nl, 3, 44, 128)
    cb = np.asarray(conv_b).reshape(nl, 1, 44, 128)
    cp = np.concatenate([cw, cb], axis=1)
    convp = f(cp.transpose(3, 0, 2, 1).reshape(128, nl * 44 * 4))
    return {
        "w_in": f(w_in), "w_out": f(w_out), "w_up": f(w_up), "w_down": f(w_down),
        "an_col": an_col, "fn_col": fn_col, "gq_rep": gq_rep, "gk_rep": gk_rep, "gfin_rep": gfin_rep,
        "relb_rep": relb_rep, "convp": convp, "ident": c["ident"], "cos": c["cos"], "sin": c["sin"],
        "masks": c["masks"],
    }


def kernel(x_prompt, x_sample, attn_norm, w_in, q_norm, k_norm, rel_bias, w_out, ffn_norm, w_up, conv_w, conv_b,
           w_down, final_norm):
    xp = np.asarray(x_prompt, dtype=np.float32)
    xs = np.asarray(x_sample, dtype=np.float32)
    nb_p = xp.shape[0]
    x_all = np.concatenate([xp, xs], axis=0)
    shared = shared_inputs(attn_norm, w_in, q_norm, k_norm, rel_bias, w_out, ffn_norm, w_up, conv_w, conv_b, w_down,
                           final_norm)
    nc = build()
    in_maps = []
    for cix in range(NCORES):
        m = dict(shared)
        m["x"] = np.ascontiguousarray(x_all[cix * SEQ_PER_CORE:(cix + 1) * SEQ_PER_CORE])
        in_maps.append(m)
    res = run_bass_kernel_spmd(nc, in_maps, core_ids=list(range(NCORES)))
    y_all = np.concatenate([np.asarray(r["y"], dtype=np.float32) for r in res.results], axis=0)
    return (np.ascontiguousarray(y_all[:nb_p]), np.ascontiguousarray(y_all[nb_p:]))
```

```python
import math
from contextlib import ExitStack
import numpy as np
import concourse.bass as bass
import concourse.mybir as mybir
from concourse.bass_utils import run_bass_kernel_spmd

F32 = mybir.dt.float32
BF16 = mybir.dt.bfloat16
ALU = mybir.AluOpType
AF = mybir.ActivationFunctionType
AX = mybir.AxisListType

S = 4096
D = 1024
NLAYER = 4
INW = 2304
DFF = 2816
NPAIR = 22
EPS = 1e-6
PADV = 1024
PADK = 1024
NCORES = 8
SEQ_PER_CORE = 3
BRANCHES = ((1, 0, 33), (4, 33, 9), (16, 69, 3))
NVCH = 117
SEM_ROT = 30000
DBG_HEADS = None
DBG_LEVEL = 9
MUL_POOL = False
MUL_POOL_MOD = 1
DBG_BSTAGE = 4
DBG_NBLK = None


class SemObj:
    def __init__(self):
        self.h = None
        self.cnt = 0


class Prog:
    def __init__(self):
        self.sems = []
        self.queues = []

    def newsem(self):
        s = SemObj()
        self.sems.append(s)
        return s


class Slot:
    def __init__(self, prog):
        self.prog = prog
        self.cur = prog.newsem()


class Q:
    def __init__(self, name, prog):
        self.name = name
        self.prog = prog
        self.ops = []
        self.cur = prog.newsem()
        self.seen = {}
        self.last = None
        prog.queues.append(self)

    def do(self, fn, sig=False):
        if sig:
            if self.cur.cnt >= SEM_ROT:
                self.cur = self.prog.newsem()
            self.cur.cnt += 1
            tk = (self.cur, self.cur.cnt)
            self.ops.append((0, fn, self.cur))
            self.last = tk
            return tk
        self.ops.append((0, fn, None))
        return None

    def wait(self, *tks):
        for tk in tks:
            if tk is None:
                continue
            so, v = tk
            if self.seen.get(so, 0) >= v:
                continue
            self.seen[so] = v
            self.ops.append((1, so, v))

    def dma(self, out, in_, slot):
        if slot.cur.cnt >= SEM_ROT:
            slot.cur = self.prog.newsem()
        slot.cur.cnt += 16
        self.ops.append((2, out, in_, slot.cur))
        return (slot.cur, slot.cur.cnt)

    def replay(self, e):
        for op in self.ops:
            if op[0] == 0:
                ins = op[1](e)
                if op[2] is not None:
                    ins.then_inc(op[2].h, 1)
            elif op[0] == 1:
                e.wait_ge(op[1].h, op[2])
            else:
                e.dma_start(out=op[1], in_=op[2]).then_inc(op[3].h, 16)


class Arena:
    def __init__(self, ap, nbytes):
        self.ap = ap
        self.nbytes = nbytes
        self.off = 0

    def reset(self):
        self.off = 0

    def _get(self, nbytes):
        nbytes = (nbytes + 63) // 64 * 64
        o = self.off
        self.off += nbytes
        assert self.off <= self.nbytes, ("arena overflow", self.off, self.nbytes)
        return self.ap[:, o // 2:(o + nbytes) // 2]

    def bf(self, n):
        return self._get(2 * n)[:, 0:n]

    def top_bf(self, n):
        o = self.nbytes - 2 * n
        return self.ap[:, o // 2:o // 2 + n]

    def f32(self, n):
        return self._get(4 * n).bitcast(F32)[:, 0:n]


def t5_bucket_np(rel):
    nb = 16
    max_exact = 8
    ret = np.where(rel > 0, nb, 0)
    n = np.abs(rel)
    lg = (np.log(np.maximum(n, 1).astype(np.float32) / np.float32(max_exact))
          / np.float32(math.log(1024 / max_exact)) * np.float32(nb - max_exact))
    large = max_exact + lg.astype(np.int32)
    large = np.minimum(large, nb - 1)
    return ret + np.where(n < max_exact, n, large)


def make_masks():
    m = np.zeros((3, 32, 128, 256), np.float32)
    p = np.arange(128)[:, None]
    i = np.arange(128)[None, :]
    for br, (d, _, _) in enumerate(BRANCHES):
        for mm in range(2):
            delta = 128 * mm - 64 + p - i
            valid = np.abs(delta) <= 64
            bk = t5_bucket_np(delta * d)
            for k in range(32):
                m[br, k, :, mm * 128:(mm + 1) * 128] = ((bk == k) & valid).astype(np.float32)
    return m


def make_rope():
    t = np.arange(S)
    row = (t // 64).astype(np.float32)
    col = (t % 64).astype(np.float32)
    n = 16
    inv = (np.float32(10000.0) ** (-np.arange(n, dtype=np.float32) / np.float32(n))).astype(np.float32)
    ang = np.concatenate([row[:, None] * inv, col[:, None] * inv], axis=-1).astype(np.float32)
    return np.cos(ang).astype(np.float32), np.sin(ang).astype(np.float32)


def build(nseq=SEQ_PER_CORE, nl=NLAYER, dbg=False, stop=None):
    nc = bass.Bass("TRN2", target_bir_lowering=False)

    def din(name, shape, dtype=F32):
        return nc.dram_tensor(name, list(shape), dtype, kind="ExternalInput").ap()

    def dscr(name, shape, dtype):
        return nc.dram_tensor(name, list(shape), dtype, kind=("ExternalOutput" if dbg else "Internal")).ap()

    x_in = din("x", [nseq, S, D])
    y_out = nc.dram_tensor("y", [nseq, S, D], F32, kind="ExternalOutput").ap()
    w_in = din("w_in", [NLAYER, D, INW])
    w_out = din("w_out", [NLAYER, D, D])
    w_up = din("w_up", [NLAYER, D, 2 * DFF])
    w_down = din("w_down", [NLAYER, DFF, D])
    an_col = din("an_col", [NLAYER, 128, 8])
    fn_col = din("fn_col", [NLAYER, 128, 8])
    gq_rep = din("gq_rep", [128, NLAYER * 64])
    gk_rep = din("gk_rep", [128, NLAYER * 64])
    gfin_rep = din("gfin_rep", [128, D])
    relb_rep = din("relb_rep", [128, 256])
    convp = din("convp", [128, NLAYER * 44 * 4])
    ident_d = din("ident", [128, 128])
    cos_d = din("cos", [S, 32])
    sin_d = din("sin", [S, 32])
    masks_d = din("masks", [3, 32, 128, 256])

    xA = dscr("xA", [S, D], F32)
    xB = dscr("xB", [S, D], F32)
    qT = dscr("qT", [1024, S], BF16)
    kT = dscr("kT", [640, S], BF16)
    vd = dscr("vd", [PADV + S + PADV, 1280], BF16)
    oT = dscr("oT", [1024, S], BF16)
    winb = dscr("winb", [NLAYER, 128, 8, INW], BF16)
    woutb = dscr("woutb", [NLAYER, 128, 8, D], BF16)
    wupb = dscr("wupb", [NLAYER, NPAIR, 128, 8, 256], BF16)
    wdnb = dscr("wdnb", [NLAYER, 128, NPAIR, D], BF16)

    prog = Prog()
    pe = Q("pe", prog)
    act = Q("act", prog)
    dve = Q("dve", prog)
    pool = Q("pool", prog)
    sp = Q("sp", prog)
    queues = [pe, act, dve, pool, sp]
    pend = []

    ARENA_BYTES = 172 * 1024
    es = ExitStack()
    arena_t = es.enter_context(nc.sbuf_tensor("arena", [128, ARENA_BYTES // 2], BF16))
    identb = es.enter_context(nc.sbuf_tensor("identb", [128, 128], BF16))
    cos_t = es.enter_context(nc.sbuf_tensor("cos_t", [128, 32 * 32], F32))
    sin_t = es.enter_context(nc.sbuf_tensor("sin_t", [128, 32 * 32], F32))
    gq_t = es.enter_context(nc.sbuf_tensor("gq_t", [128, NLAYER * 64], F32))
    gk_t = es.enter_context(nc.sbuf_tensor("gk_t", [128, NLAYER * 64], F32))
    gfin_t = es.enter_context(nc.sbuf_tensor("gfin_t", [128, D], F32))
    convp_t = es.enter_context(nc.sbuf_tensor("convp_t", [128, NLAYER * 44 * 4], F32))
    ancol_t = es.enter_context(nc.sbuf_tensor("ancol_t", [128, NLAYER * 8], F32))
    fncol_t = es.enter_context(nc.sbuf_tensor("fncol_t", [128, NLAYER * 8], F32))
    E_t = es.enter_context(nc.sbuf_tensor("E_t", [128, 24 * 256], BF16))
    st_t = es.enter_context(nc.sbuf_tensor("st_t", [128, 256], F32))
    ps_t = es.enter_context(nc.psum_tensor("ps", [128, 4096], F32))

    A = Arena(arena_t[:], ARENA_BYTES)
    ident = identb[:]
    cosv = cos_t[:].rearrange("p (i c) -> p i c", c=32)
    sinv = sin_t[:].rearrange("p (i c) -> p i c", c=32)
    convv = convp_t[:].rearrange("p (l c f) -> p l c f", l=NLAYER, f=4)
    Ev = E_t[:]
    st = st_t[:]

    def bank(b, n=512):
        return ps_t[:, b * 512:b * 512 + n]

    def bank_bf(b, nb=1):
        return ps_t[:, b * 512:(b + nb) * 512].bitcast(BF16)

    def MM(out, lhsT, rhs, start, stop, sig=False):
        return pe.do(lambda e: e.matmul(out, lhsT=lhsT, rhs=rhs, start=start, stop=stop), sig)

    def TR(out, in_, sig=False):
        k = in_.shape[0]
        return pe.do(lambda e: e.transpose(out, in_, ident[0:k, 0:k]), sig)

    def ACT(out, in_, func, sig=False, **kw):
        return act.do(lambda e: e.activation(out, in_, func, **kw), sig)

    def TS(q, out, in0, s1, s2, op0, op1=None, sig=False):
        if op1 is None:
            return q.do(lambda e: e.tensor_scalar(out, in0, s1, None, op0), sig)
        return q.do(lambda e: e.tensor_scalar(out, in0, s1, s2, op0, op1), sig)

    def TT(q, out, in0, in1, op, sig=False):
        return q.do(lambda e: e.tensor_tensor(out, in0, in1, op), sig)

    def STT(q, out, in0, scalar, in1, op0, op1, sig=False):
        return q.do(lambda e: e.scalar_tensor_tensor(out, in0, scalar, in1, op0, op1), sig)

    def CP(q, out, in_, sig=False):
        return q.do(lambda e: e.tensor_copy(out, in_), sig)

    def MSET(q, ap, val, sig=False):
        return q.do(lambda e: e.memset(ap, val), sig)

    def barrier():
        tks = [q.last for q in queues if q.last is not None] + list(pend)
        for q in queues:
            q.wait(*tks)
        pend.clear()

    def rstd_chain(ssq_ap, tmp1, tmp2, out_ap, inv_n, wait_tk):
        dve.wait(wait_tk)
        t = TS(dve, tmp1, ssq_ap, inv_n, EPS, ALU.mult, ALU.add, sig=True)
        act.wait(t)
        t = ACT(tmp2, tmp1, AF.Ln, sig=True)
        act.wait(t)
        return ACT(out_ap, tmp2, AF.Exp, sig=True, scale=-0.5)

    def setup():
        A.reset()
        idf = A.f32(128)
        eb = A.f32(256)
        ebr = A.f32(256)
        mk = [A.f32(32 * 256) for _ in range(2)]
        eacc = [A.f32(256) for _ in range(2)]
        s0 = Slot(prog)
        sp.dma(idf, ident_d, s0)
        sp.dma(cosv, cos_d.rearrange("(i p) c -> p i c", p=128), s0)
        sp.dma(sinv, sin_d.rearrange("(i p) c -> p i c", p=128), s0)
        sp.dma(gq_t[:], gq_rep, s0)
        sp.dma(gk_t[:], gk_rep, s0)
        sp.dma(gfin_t[:], gfin_rep, s0)
        sp.dma(convp_t[:], convp, s0)
        sp.dma(ancol_t[:].rearrange("p (l k) -> p l k", l=NLAYER), an_col.rearrange("l p k -> p l k"), s0)
        sp.dma(fncol_t[:].rearrange("p (l k) -> p l k", l=NLAYER), fn_col.rearrange("l p k -> p l k"), s0)
        t0 = sp.dma(ebr, relb_rep, s0)
        dve.wait(t0)
        act.wait(t0)
        pool.wait(t0)
        t_id = CP(dve, ident, idf, sig=True)
        t_eb = ACT(eb, ebr, AF.Exp, sig=True)
        dve.wait(t_eb)
        pool.wait(t_eb)
        ms = [Slot(prog), Slot(prog)]
        tk_m = {}
        tk_use = {}
        for br in range(3):
            sp.wait(tk_use.get(br - 2))
            tk_m[br] = sp.dma(mk[br % 2].rearrange("p (k c) -> p k c", k=32),
                              masks_d[br].rearrange("k p c -> p k c"), ms[br % 2])
            last = []
            for h in range(8):
                q = dve
                q.wait(tk_m[br])
                ea = eacc[h % 2]
                mv = mk[br % 2].rearrange("p (k c) -> p k c", k=32)
                nzk = [k for k in range(32) if host_consts()["masks"][br, k].any()]
                for ki, k in enumerate(nzk):
                    col = eb[:, k * 8 + h:k * 8 + h + 1]
                    if ki == 0:
                        TS(q, ea, mv[:, k, :], col, None, ALU.mult)
                    else:
                        STT(q, ea, mv[:, k, :], col, ea, ALU.mult, ALU.add)
                t = CP(q, Ev[:, (h * 3 + br) * 256:(h * 3 + br + 1) * 256], ea, sig=True)
                if h >= 6:
                    last.append(t)
            sp.wait(*last)
            tk_use[br] = last[-1]
        z = A.bf(1280)
        t = MSET(pool, z, 0.0, sig=True)
        pool.wait(t)
        zs = Slot(prog)
        for j in range(PADV // 128):
            pend.append(pool.dma(vd[j * 128:(j + 1) * 128, :], z, zs))
            pend.append(pool.dma(vd[PADV + S + j * 128:PADV + S + (j + 1) * 128, :], z, zs))
        barrier()

    def prepass():
        A.reset()
        stg = [A.f32(5632) for _ in range(2)]
        ob = [A.bf(5632) for _ in range(2)]
        ls = [Slot(prog), Slot(prog)]
        ss = [Slot(prog), Slot(prog)]
        items = []
        for l in range(nl):
            for k in range(8):
                items.append((w_in[l, k * 128:(k + 1) * 128, :], INW, ancol_t[:, l * 8 + k:l * 8 + k + 1],
                              lambda o, l=l, k=k: [(winb[l, :, k, :], o)]))
            for k in range(8):
                items.append((w_out[l, k * 128:(k + 1) * 128, :], D, None,
                              lambda o, l=l, k=k: [(woutb[l, :, k, :], o)]))
            for k in range(8):
                items.append((w_up[l, k * 128:(k + 1) * 128, :], 2 * DFF, fncol_t[:, l * 8 + k:l * 8 + k + 1],
                              lambda o, l=l, k=k: [
                                  (wupb[l].rearrange("c p k (g j) -> p k g c j", g=2)[:, k, g],
                                   o[:, g * DFF:(g + 1) * DFF].rearrange("p (c j) -> p c j", j=128))
                                  for g in range(2)]))
            for c in range(NPAIR):
                items.append((w_down[l, c * 128:(c + 1) * 128, :], D, None,
                              lambda o, l=l, c=c: [(wdnb[l, :, c, :], o)]))
        tk_ld = {}
        tk_cv = {}
        tk_st = {}

        def load(i):
            src, n, _, _ = items[i]
            sp.wait(tk_cv.get(i - 2))
            tk_ld[i] = sp.dma(stg[i % 2][:, 0:n], src, ls[i % 2])

        load(0)
        for i in range(len(items)):
            if i + 1 < len(items):
                load(i + 1)
            src, n, sc, dstf = items[i]
            q = dve if i % 2 == 0 else act
            q.wait(tk_ld[i], tk_st.get(i - 2))
            if q is dve:
                if sc is None:
                    tk_cv[i] = CP(dve, ob[i % 2][:, 0:n], stg[i % 2][:, 0:n], sig=True)
                else:
                    tk_cv[i] = TS(dve, ob[i % 2][:, 0:n], stg[i % 2][:, 0:n], sc, None, ALU.mult, sig=True)
            else:
                if sc is None:
                    tk_cv[i] = ACT(ob[i % 2][:, 0:n], stg[i % 2][:, 0:n], AF.Copy, sig=True)
                else:
                    tk_cv[i] = ACT(ob[i % 2][:, 0:n], stg[i % 2][:, 0:n], AF.Copy, sig=True, scale=sc)
            pool.wait(tk_cv[i])
            for d_ap, s_ap in dstf(ob[i % 2][:, 0:n]):
                tk_st[i] = pool.dma(d_ap, s_ap, ss[i % 2])
        pend.extend(tk_st.values())
        barrier()

    slotW = Slot(prog)
    prefetch = {}
    _slots = {}

    def SL(name):
        if name not in _slots:
            _slots[name] = Slot(prog)
        return _slots[name]

    def phaseP(s, l):
        A.reset()
        xt = [A.f32(1024) for _ in range(2)]
        junk = A.bf(1024)
        hb = A.bf(1024)
        hT = A.bf(1024)
        win = A.bf(8 * INW)
        tq = A.f32(640)
        sq = A.f32(640)
        r1 = A.f32(320)
        r2 = A.f32(320)
        G = A.f32(640)
        qk = [A.bf(1664) for _ in range(2)]
        vt = [A.bf(1280) for _ in range(2)]
        stg = [A.bf(13 * 512) for _ in range(2)]
        barrier()
        xsrc = x_in[s] if l == 0 else xA
        winv = win.rearrange("p (k n) -> p k n", k=8)
        tw = sp.dma(winv, winb[l], slotW)
        Gv = G.rearrange("p (h c) -> p h c", h=10)
        CP(pool, Gv[:, 0:8, :], gq_t[:, l * 64:(l + 1) * 64].unsqueeze(1).to_broadcast([128, 8, 64]))
        CP(pool, Gv[:, 8:10, :], gk_t[:, l * 64:(l + 1) * 64].unsqueeze(1).to_broadcast([128, 2, 64]))
        for b in range(2):
            tG = MSET(pool, vt[b].rearrange("p (h e) -> p h e", h=10)[:, :, 64:128], 1.0, sig=True)
        dve.wait(tG)
        psT = bank_bf(0)
        psQ = bank_bf(6, 2)
        groups = ((0, 512, 1), (512, 768, 2), (768, 1280, 3), (1280, 1792, 4), (1792, 2304, 5))
        sx = [SL("P.sx0"), SL("P.sx1")]
        sv = [SL("P.sv0"), SL("P.sv1")]
        sg = [SL("P.sg0"), SL("P.sg1")]
        tk_x, tk_sq, tk_hb, tk_tr, tk_hT, tk_proj = {}, {}, {}, {}, {}, {}
        tk_eva, tk_evd, tk_rope, tk_tr2, tk_stg, tk_sd, tk_vst = {}, {}, {}, {}, {}, {}, {}

        def load(i):
            sp.wait(tk_hb.get(i - 2), tk_sq.get(i - 2))
            tk_x[i] = sp.dma(xt[i % 2], xsrc[i * 128:(i + 1) * 128, :], sx[i % 2])

        def c(i, j, n=1):
            c0 = (i % 4) * 48
            return st[:, c0 + j:c0 + j + n]

        t_tq, t_qb = {}, {}

        def stageN(i):
            sl = i % 2
            act.wait(tk_x[i])
            tk_sq[i] = ACT(junk, xt[sl], AF.Square, sig=True, accum_out=c(i, 0))
            t_rs = rstd_chain(c(i, 0), c(i, 1), c(i, 2), c(i, 3), 1.0 / D, tk_sq[i])
            dve.wait(t_rs, tk_tr.get(i - 1))
            tk_hb[i] = TS(dve, hb, xt[sl], c(i, 3), None, ALU.mult, sig=True)

        def stageT(i):
            pe.wait(tk_hb[i], tk_hT.get(i - 1))
            for k in range(8):
                t = TR(psT[:, k * 128:(k + 1) * 128], hb[:, k * 128:(k + 1) * 128], sig=(k == 7))
            tk_tr[i] = t
            dve.wait(tk_tr[i], tk_proj.get(i - 1))
            tk_hT[i] = CP(dve, hT, psT, sig=True)
            pe.wait(tk_hT[i], tw, tk_eva.get(i - 1), tk_evd.get(i - 1))
            for k in range(8):
                for gi, (a0, a1, bk) in enumerate(groups):
                    t = MM(bank(bk, a1 - a0), hT[:, k * 128:(k + 1) * 128], winv[:, k, a0:a1], k == 0, k == 7,
                           sig=(k == 7 and gi == 4))
            tk_proj[i] = t

        def stageE1(i):
            sl = i % 2
            qkc = qk[i % 2]
            act.wait(tk_proj[i], tk_rope.get(i - 1))
            ACT(tq[:, 0:512], bank(1), AF.Copy)
            t_tq[i] = ACT(tq[:, 512:640], bank(2, 128), AF.Copy, sig=True)
            act.wait(tk_tr2.get(i - 2))
            ACT(qkc[:, 640:1152], bank(3), AF.Copy)
            t_qb[i] = ACT(qkc[:, 1152:1664], bank(4), AF.Copy, sig=True)
            tk_eva[i] = t_qb[i]
            dve.wait(tk_proj[i], tk_vst.get(i - 2))
            vt3 = vt[sl].rearrange("p (h e) -> p h e", h=10)
            CP(dve, vt3[:, 0:2, 0:64], bank(2, 256)[:, 128:256].rearrange("p (h e) -> p h e", h=2))
            tk_evd[i] = CP(dve, vt3[:, 2:10, 0:64], bank(5).rearrange("p (h e) -> p h e", h=8), sig=True)
            pool.wait(tk_evd[i])
            tk_vst[i] = pool.dma(vd[PADV + i * 128:PADV + (i + 1) * 128, :], vt[sl], sv[sl])
            pend.append(tk_vst[i])

        def stageE2(i):
            qkc = qk[i % 2]
            dve.wait(t_tq[i])
            tq3 = tq.rearrange("p (h c) -> p h c", h=10)
            TT(dve, sq, tq, tq, ALU.mult)
            t_hs = dve.do(lambda e, o=c(i, 4, 10), i_=sq.rearrange("p (h c) -> p h c", h=10): e.tensor_reduce(
                o, i_, AX.X, ALU.add), sig=True)
            t_rsh = rstd_chain(c(i, 4, 10), c(i, 14, 10), c(i, 24, 10), c(i, 34, 10), 1.0 / 64, t_hs)
            dve.wait(t_rsh, tk_tr2.get(i - 2))
            TT(dve, tq3, tq3, c(i, 34, 10).unsqueeze(2).to_broadcast([128, 10, 64]), ALU.mult)
            TT(dve, tq3, tq3, Gv, ALU.mult)
            x0 = tq3[:, :, 0:64:2]
            x1 = tq3[:, :, 1:64:2]
            cb = cosv[:, i, :].unsqueeze(1).to_broadcast([128, 10, 32])
            sb = sinv[:, i, :].unsqueeze(1).to_broadcast([128, 10, 32])
            r1v = r1.rearrange("p (h c) -> p h c", h=10)
            r2v = r2.rearrange("p (h c) -> p h c", h=10)
            qk3 = qkc[:, 0:640].rearrange("p (h c) -> p h c", h=10)
            TT(dve, r1v, x0, cb, ALU.mult)
            TT(dve, r2v, x1, sb, ALU.mult)
            TT(dve, qk3[:, :, 0:64:2], r1v, r2v, ALU.subtract)
            TT(dve, r1v, x0, sb, ALU.mult)
            TT(dve, r2v, x1, cb, ALU.mult)
            tk_rope[i] = TT(dve, qk3[:, :, 1:64:2], r1v, r2v, ALU.add, sig=True)

        def stageT2(i):
            qkc = qk[i % 2]
            pe.wait(tk_rope[i], t_qb[i], tk_stg.get(i - 1))
            for j in range(13):
                t = TR(psQ[:, j * 128:(j + 1) * 128], qkc[:, j * 128:(j + 1) * 128], sig=(j == 12))
            tk_tr2[i] = t
            g = i // 4
            dve.wait(tk_tr2[i], tk_sd.get(g - 2))
            sg3 = stg[g % 2].rearrange("p (j t) -> p j t", j=13)
            tk_stg[i] = CP(dve, sg3[:, :, (i % 4) * 128:(i % 4 + 1) * 128],
                           psQ[:, 0:1664].rearrange("p (j t) -> p j t", j=13), sig=True)
            if i % 4 == 3:
                pool.wait(tk_stg[i])
                t0 = g * 512
                pool.dma(qT[0:512, t0:t0 + 512].rearrange("(j p) t -> p j t", p=128), sg3[:, 0:4, :], sg[g % 2])
                pool.dma(kT[0:128, t0:t0 + 512], sg3[:, 4, :], sg[g % 2])
                pool.dma(qT[512:1024, t0:t0 + 512].rearrange("(j p) t -> p j t", p=128), sg3[:, 5:9, :], sg[g % 2])
                tk_sd[g] = pool.dma(kT[128:640, t0:t0 + 512].rearrange("(j p) t -> p j t", p=128), sg3[:, 9:13, :],
                                    sg[g % 2])
                pend.append(tk_sd[g])

        load(0)
        load(1)
        stageN(0)
        stageT(0)
        for i in range(32):
            if i + 2 < 32:
                load(i + 2)
            stageE1(i)
            if i + 1 < 32:
                stageN(i + 1)
                stageT(i + 1)
            stageE2(i)
            stageT2(i)

    def phaseATT(s, l):
        A.reset()
        qh = [A.bf(S) for _ in range(2)]
        kh = [A.bf(PADK + S + PADK) for _ in range(2)]
        vh = [A.bf(NVCH * 128) for _ in range(2)]
        pT = [A.bf(1024) for _ in range(3)]
        NSB = 4
        pex = [A.f32(512) for _ in range(NSB)]
        pTb = [A.bf(512) for _ in range(NSB)]
        acc = A.f32(S)
        oh = [A.bf(S) for _ in range(2)]
        oh2 = [A.bf(S) for _ in range(2)]
        rd = A.f32(512)
        barrier()
        for b in range(2):
            MSET(pool, kh[b][:, 0:PADK], 0.0)
            tkz = MSET(pool, kh[b][:, PADK + S:PADK + S + PADK], 0.0, sig=True)
        sp.wait(tkz)
        pe.wait(tkz)
        sh = [SL("A.sh0"), SL("A.sh1")]
        so = [SL("A.so0"), SL("A.so1")]
        heads = [("A", i) for i in range(4)] + [("B", h) for h in range(8)]
        if DBG_HEADS is not None:
            heads = DBG_HEADS
        tk_head, tk_done, tk_ohst = {}, {}, {}
        psS2 = [ps_t[:, 0:1024], ps_t[:, 1024:2048]]
        psOa = [bank(4), bank(6)]
        psOb = [bank(5), bank(7)]
        psSB = [bank(0), bank(1), bank(2), bank(3)]
        psOB = [bank(4, 256), bank(5, 256), bank(6, 256), bank(7, 256)]
        cnt = {"c": 0, "q": 0, "b": 0}
        tk_qk, tk_exp, tk_pv, tk_norm = {}, {}, {}, {}
        tk_s1, tk_eB, tk_mB, tk_pvB, tk_accB = {}, {}, {}, {}, {}

        def loadhead(n):
            typ, h = heads[n]
            b = n % 2
            sp.wait(tk_done.get(n - 2))
            if typ == "A":
                g = h // 2
                sp.dma(qh[b][:, :], qT[h * 128:(h + 1) * 128, :], sh[b])
                sp.dma(kh[b][0:64, PADK:PADK + S], kT[g * 64:(g + 1) * 64, :], sh[b])
                sp.dma(kh[b][64:128, PADK:PADK + S], kT[g * 64:(g + 1) * 64, :], sh[b])
                tk = sp.dma(vh[b][:, 0:32 * 128].rearrange("p (m e) -> p m e", e=128),
                            vd[PADV:PADV + S, g * 128:(g + 1) * 128].rearrange("(m p) e -> p m e", p=128), sh[b])
            else:
                sp.dma(qh[b][0:64, :], qT[512 + h * 64:512 + (h + 1) * 64, :], sh[b])
                sp.dma(kh[b][0:64, PADK:PADK + S], kT[128 + h * 64:128 + (h + 1) * 64, :], sh[b])
                for (d, base, nm) in BRANCHES:
                    for r in range(d):
                        r0 = PADV - 64 * d + r
                        src = vd[r0:r0 + (128 * nm - 1) * d + 1:d, (2 + h) * 128:(3 + h) * 128]
                        tk = sp.dma(vh[b][:, (base + r * nm) * 128:(base + (r + 1) * nm) * 128].rearrange(
                            "p (m e) -> p m e", e=128), src.rearrange("(m p) e -> p m e", p=128), sh[b])
            tk_head[n] = tk

        def normalize(src_num, src_den, dst, sig):
            dve.do(lambda e: e.reciprocal(rd[64:128, :], src_den))
            CP(dve, rd[0:64, :], rd[64:128, :])
            return TT(dve, dst, src_num, rd[0:64, :], ALU.mult, sig=sig)

        def headA(n, i):
            b = n % 2
            jobs = [(qt, c) for qt in range(8) for c in range(32)]
            base = cnt["c"]
            qbase = cnt["q"]

            def QK(j):
                qt, c = jobs[j]
                gi = base + j
                sl = gi % 2
                pe.wait(tk_head[n], tk_exp.get(gi - 2))
                if j == 0:
                    pe.wait(dve.last)
                MM(psS2[sl][:, 0:512], kh[b][0:64, PADK + c * 128:PADK + (c + 1) * 128],
                   qh[b][0:64, qt * 512:(qt + 1) * 512], True, True)
                tk_qk[gi] = MM(psS2[sl][:, 512:1024], kh[b][64:128, PADK + c * 128:PADK + (c + 1) * 128],
                               qh[b][64:128, qt * 512:(qt + 1) * 512], True, True, sig=True)

            QK(0)
            for j, (qt, c) in enumerate(jobs):
                gi = base + j
                gq = qbase + qt
                if j + 1 < len(jobs):
                    QK(j + 1)
                act.wait(tk_qk[gi], tk_pv.get(gi - 3))
                tk_exp[gi] = ACT(pT[gi % 3], psS2[gi % 2], AF.Exp, sig=True, scale=0.125)
                pe.wait(tk_exp[gi])
                if c == 0:
                    pe.wait(tk_norm.get(gq - 2))
                MM(psOa[gq % 2], vh[b][:, c * 128:(c + 1) * 128], pT[gi % 3][:, 0:512], c == 0, c == 31)
                tk_pv[gi] = MM(psOb[gq % 2], vh[b][:, c * 128:(c + 1) * 128], pT[gi % 3][:, 512:1024], c == 0,
                               c == 31, sig=True)
                if c == 31:
                    dve.wait(tk_pv[gi], tk_ohst.get(n - 2))
                    normalize(psOa[gq % 2][0:64, :], psOa[gq % 2][64:128, :],
                              oh[b][0:64, qt * 512:(qt + 1) * 512], False)
                    tk_norm[gq] = normalize(psOb[gq % 2][0:64, :], psOb[gq % 2][64:128, :],
                                            oh2[b][0:64, qt * 512:(qt + 1) * 512], True)
            cnt["c"] += len(jobs)
            cnt["q"] += 8
            tk_done[n] = tk_pv[base + len(jobs) - 1]
            pool.wait(tk_norm[qbase + 7])
            pool.dma(oT[i * 128:i * 128 + 64, :], oh[b][0:64, :], so[b])
            tk_ohst[n] = pool.dma(oT[i * 128 + 64:i * 128 + 128, :], oh2[b][0:64, :], so[b])
            pend.append(tk_ohst[n])

        def headB(n, h):
            b = n % 2
            items = []
            for br, (d, vbase, nm) in enumerate(BRANCHES):
                if DBG_LEVEL < 7 and br != DBG_LEVEL:
                    continue
                for r in range(d):
                    for bp in range(16 // d):
                        items.append((br, d, vbase, nm, r, bp))
            if DBG_LEVEL == 7:
                items = items[:1]
            base = cnt["b"]
            if n == 0 or heads[n - 1][0] == "A":
                pe.wait(act.last, dve.last)

            def S1(j):
                br, d, vbase, nm, r, bp = items[j]
                gi = base + j
                pe.wait(tk_head[n], tk_eB.get(gi - NSB))
                bk = 2 * bp
                q0 = r + d * 128 * bk
                kc = [PADK + r + d * (128 * (bk + m) - 64) for m in range(3)]
                MM(psSB[gi % NSB][:, 0:128], kh[b][0:64, kc[0]:kc[0] + 127 * d + 1:d],
                   qh[b][0:64, q0:q0 + 127 * d + 1:d], True, True)
                MM(psSB[gi % NSB][:, 128:384], kh[b][0:64, kc[1]:kc[1] + 127 * d + 1:d],
                   qh[b][0:64, q0:q0 + 255 * d + 1:d], True, True)
                t = MM(psSB[gi % NSB][:, 384:512], kh[b][0:64, kc[2]:kc[2] + 127 * d + 1:d],
                       qh[b][0:64, q0 + 128 * d:q0 + 255 * d + 1:d], True, True, sig=True)
                tk_s1[gi] = t

            def EXPMUL(j):
                br, d, vbase, nm, r, bp = items[j]
                gi = base + j
                act.wait(tk_s1[gi], tk_mB.get(gi - NSB))
                tk_eB[gi] = ACT(pex[gi % NSB], psSB[gi % NSB], AF.Exp, sig=True, scale=0.125)
                mq = pool if (MUL_POOL and gi % MUL_POOL_MOD == 0) else dve
                mq.wait(tk_eB[gi], tk_pvB.get(gi - NSB))
                Eb = Ev[:, (h * 3 + br) * 256:(h * 3 + br + 1) * 256].unsqueeze(1).to_broadcast([128, 2, 256])
                tk_mB[gi] = TT(mq, pTb[gi % NSB].rearrange("p (a c) -> p a c", a=2),
                               pex[gi % NSB].rearrange("p (a c) -> p a c", a=2), Eb, ALU.mult, sig=True)

            for jj in range(min(3, len(items))):
                S1(jj)
            EXPMUL(0)
            if len(items) > 1:
                EXPMUL(1)
            for j, (br, d, vbase, nm, r, bp) in enumerate(items):
                gi = base + j
                if j + 3 < len(items):
                    S1(j + 3)
                if j + 2 < len(items):
                    EXPMUL(j + 2)
                pe.wait(tk_mB[gi], tk_accB.get(gi - NSB))
                ch = vbase + r * nm + 2 * bp
                MM(psOB[gi % NSB][:, 0:256], vh[b][:, (ch + 1) * 128:(ch + 2) * 128], pTb[gi % NSB][:, 128:384], True, False)
                MM(psOB[gi % NSB][:, 0:128], vh[b][:, ch * 128:(ch + 1) * 128], pTb[gi % NSB][:, 0:128], False, True)
                t = MM(psOB[gi % NSB][:, 128:256], vh[b][:, (ch + 2) * 128:(ch + 3) * 128], pTb[gi % NSB][:, 384:512],
                       False, True, sig=True)
                tk_pvB[gi] = t
                dve.wait(t)
                q0 = r + d * 256 * bp
                accv = acc[:, q0:q0 + 255 * d + 1:d]
                if br == 0:
                    tk_accB[gi] = CP(dve, accv, psOB[gi % NSB], sig=True)
                else:
                    tk_accB[gi] = TT(dve, accv, accv, psOB[gi % NSB], ALU.add, sig=True)
            cnt["b"] += len(items)
            tk_done[n] = tk_pvB[base + len(items) - 1]
            dve.wait(tk_ohst.get(n - 2))
            for qt in range(8):
                t = normalize(acc[0:64, qt * 512:(qt + 1) * 512], acc[64:128, qt * 512:(qt + 1) * 512],
                              oh[b][0:64, qt * 512:(qt + 1) * 512], qt == 7)
            pool.wait(t)
            tk_ohst[n] = pool.dma(oT[512 + h * 64:512 + (h + 1) * 64, :], oh[b][0:64, :], so[b])
            pend.append(tk_ohst[n])

        loadhead(0)
        for n, (typ, h) in enumerate(heads):
            if n + 1 < len(heads):
                loadhead(n + 1)
            if typ == "A":
                headA(n, h)
            else:
                headB(n, h)

    def phaseO(s, l):
        A.reset()
        wout = A.bf(8 * D)
        ot = [A.bf(8 * 512) for _ in range(2)]
        xt = [A.f32(D) for _ in range(3)]
        barrier()
        woutv = wout.rearrange("p (k n) -> p k n", k=8)
        tw = sp.dma(woutv, woutb[l], slotW)
        prefetch["wdn"] = sp.dma(A.top_bf(NPAIR * D).rearrange("p (c n) -> p c n", c=NPAIR), wdnb[l], SL("F.wd"))
        xsrc = x_in[s] if l == 0 else xA
        sot = [SL("O.sot0"), SL("O.sot1")]
        sx = [SL("O.sx%d" % i) for i in range(3)]
        sxs = [SL("O.sxs%d" % i) for i in range(3)]
        psX = [ps_t[:, 0:1024], ps_t[:, 1024:2048]]
        tk_ot, tk_x, tk_mm, tk_add, tk_xst = {}, {}, {}, {}, {}
        for g in range(8):
            sp.wait(tk_mm.get((g - 2) * 4 + 3))
            otv = ot[g % 2].rearrange("p (k t) -> p k t", k=8)
            tk_ot[g] = sp.dma(otv, oT[:, g * 512:(g + 1) * 512].rearrange("(k p) t -> p k t", p=128), sot[g % 2])
            for j in range(4):
                i = g * 4 + j
                sp.wait(tk_xst.get(i - 3))
                tk_x[i] = sp.dma(xt[i % 3], xsrc[i * 128:(i + 1) * 128, :], sx[i % 3])
                pe.wait(tk_ot[g], tw, tk_add.get(i - 2))
                for n2 in range(2):
                    for k in range(8):
                        t = MM(psX[i % 2][:, n2 * 512:(n2 + 1) * 512], otv[:, k, j * 128:(j + 1) * 128],
                               woutv[:, k, n2 * 512:(n2 + 1) * 512], k == 0, k == 7, sig=(n2 == 1 and k == 7))
                tk_mm[i] = t
                dve.wait(t, tk_x[i])
                TT(dve, xt[i % 3][:, 0:512], xt[i % 3][:, 0:512], psX[i % 2][:, 0:512], ALU.add)
                tk_add[i] = TT(dve, xt[i % 3][:, 512:1024], xt[i % 3][:, 512:1024], psX[i % 2][:, 512:1024], ALU.add,
                               sig=True)
                pool.wait(tk_add[i])
                tk_xst[i] = pool.dma(xB[i * 128:(i + 1) * 128, :], xt[i % 3], sxs[i % 3])
                pend.append(tk_xst[i])

    def phaseF(s, l, last):
        A.reset()
        xt5 = [A.f32(5 * D) for _ in range(2)]
        junk = A.bf(D)
        hb = A.bf(5 * D)
        h2T = A.bf(8 * 520)
        wup = [A.bf(8 * 256) for _ in range(3)]
        wdn = A.top_bf(NPAIR * D)
        U = [[A.f32(516) for _ in range(2)] for _ in range(2)]
        cg = [A.f32(512) for _ in range(2)]
        sgl = [A.f32(512) for _ in range(2)]
        cv = A.f32(512)
        aT = A.bf(NPAIR * 512)
        barrier()
        wdnv = wdn.rearrange("p (c n) -> p c n", c=NPAIR)
        assert A.off <= A.nbytes - 2 * NPAIR * D
        twd = prefetch["wdn"]
        h2Tv = h2T.rearrange("p (k t) -> p k t", k=8)
        aTv = aT.rearrange("p (c t) -> p c t", c=NPAIR)
        hb3 = hb.rearrange("p (j d) -> p j d", j=5)
        sxl = [SL("F.sxl0"), SL("F.sxl1")]
        sws = [SL("F.sws%d" % i) for i in range(3)]
        sxs = [SL("F.sxs0"), SL("F.sxs1")]
        psT = [bank_bf(0), bank_bf(1)]
        psHs = [bank(0, 16), bank(1, 16)]
        psGV = [(bank(2), bank(3)), (bank(4), bank(5))]
        psD = [ps_t[:, 3072:4096], ps_t[:, 1024:2048]]
        dst = y_out[s] if last else xA
        tk_x, tk_st, tk_w, tk_pair, tk_ev, tk_a, tk_dn, tk_h2 = {}, {}, {}, {}, {}, {}, {}, {}
        gpc = {"p": 0, "d": 0, "t": 0}

        def loadx(T):
            slx = T % 2
            t0 = T * 512
            x5 = xt5[slx].rearrange("p (j d) -> p j d", j=5)
            pool.wait(*tk_st.get(T - 2, ()))
            tz = MSET(pool, x5[0:2, 4, :], 0.0, sig=True)
            sp.wait(tz)
            sp.wait(*tk_st.get(T - 2, ()))
            tk = sp.dma(x5[:, 0:4, :], xB[t0:t0 + 512, :].rearrange("(j p) d -> p j d", p=128), sxl[slx])
            if T > 0:
                tk = sp.dma(x5[0:1, 4, :], xB[t0 - 1:t0, :], sxl[slx])
            if T < 7:
                tk = sp.dma(x5[1:2, 4, :], xB[t0 + 512:t0 + 513, :], sxl[slx])
            tk_x[T] = tk

        def loadw(T, c):
            gi = T * NPAIR + c
            sp.wait(tk_pair.get(gi - 3))
            tk_w[gi] = sp.dma(wup[gi % 3].rearrange("p (k n) -> p k n", k=8), wupb[l, c], sws[gi % 3])

        loadx(0)
        for T in range(8):
            if T + 1 < 8:
                loadx(T + 1)
            slx = T % 2
            x5 = xt5[slx].rearrange("p (j d) -> p j d", j=5)
            c0 = (T % 2) * 64

            def c(j, n=1):
                return st[:, c0 + j:c0 + j + n]
            loadw(T, 0)
            loadw(T, 1)
            act.wait(tk_x[T])
            for j in range(5):
                npp = 128 if j < 4 else 2
                t = ACT(junk[0:npp, :], x5[0:npp, j, :], AF.Square, sig=(j == 4), accum_out=st[0:npp, c0 + j:c0 + j + 1])
            t_rs = rstd_chain(c(0, 5), c(8, 5), c(16, 5), c(24, 5), 1.0 / D, t)
            dve.wait(t_rs, tk_h2.get(T - 1))
            for j in range(5):
                npp = 128 if j < 4 else 2
                t = TS(dve, hb3[0:npp, j, :], x5[0:npp, j, :], st[0:npp, c0 + 24 + j:c0 + 25 + j], None, ALU.mult,
                       sig=True)
            t_hb = t
            pe.wait(t_hb, tk_dn.get(T - 1), tk_dn.get(("d", gpc["d"] - 1)), tk_dn.get(("d", gpc["d"] - 2)),
                    *tk_ev.get(T - 1, ()))
            tcs = []
            for j in range(5):
                gt = gpc["t"]
                gpc["t"] += 1
                npp = 128 if j < 4 else 2
                pe.wait(tk_h2.get(("c", gt - 2)))
                for k in range(8):
                    t = TR(psT[gt % 2][:, k * 128:k * 128 + npp], hb3[0:npp, j, k * 128:(k + 1) * 128], sig=(k == 7))
                dve.wait(t)
                if j < 4:
                    t = CP(dve, h2Tv[:, :, j * 128:(j + 1) * 128],
                           psT[gt % 2].rearrange("p (k t) -> p k t", k=8), sig=True)
                else:
                    t = CP(dve, h2Tv[:, :, 512:514],
                           psT[gt % 2].rearrange("p (k t) -> p k t", k=8)[:, :, 0:2], sig=True)
                tk_h2[("c", gt)] = t
            t_h2 = t
            tk_ev[T] = []
            for cpi in range(NPAIR):
                gi = T * NPAIR + cpi
                if cpi + 2 < NPAIR:
                    loadw(T, cpi + 2)
                par = gi % 2
                wv = wup[gi % 3].rearrange("p (k n) -> p k n", k=8)
                pG, pV = psGV[par]
                hoff = 0
                psH = psHs[par]
                pe.wait(tk_w[gi], t_h2, tk_a.get(gi - 2))
                for (pp, n0, ho) in ((pG, 0, hoff), (pV, 128, hoff + 2)):
                    for k in range(8):
                        MM(pp, wv[:, k, n0:n0 + 128], h2Tv[:, k, 0:512], k == 0, k == 7)
                        t = MM(psH[:, ho:ho + 2], wv[:, k, n0:n0 + 128], h2Tv[:, k, 512:514], k == 0, k == 7,
                               sig=(k == 7 and n0 == 128))
                tk_pair[gi] = t
                Ug, Uv = U[par]
                act.wait(t, tk_a.get(gi - 2))
                ACT(Ug[:, 1:513], pG, AF.Copy)
                ACT(Ug[:, 0:514:513], psH[:, hoff:hoff + 2], AF.Copy)
                ACT(Uv[:, 1:513], pV, AF.Copy)
                t_ev = ACT(Uv[:, 0:514:513], psH[:, hoff + 2:hoff + 4], AF.Copy, sig=True)
                tk_ev[T] = [t_ev]
                dve.wait(t_ev)
                cgp = cg[par]
                for (Ux, co, dstc) in ((Ug, cpi, cgp), (Uv, NPAIR + cpi, cv)):
                    TS(dve, dstc, Ux[:, 1:513], convv[:, l, co, 1:2], convv[:, l, co, 3:4], ALU.mult, ALU.add)
                    STT(dve, dstc, Ux[:, 0:512], convv[:, l, co, 0:1], dstc, ALU.mult, ALU.add)
                    t = STT(dve, dstc, Ux[:, 2:514], convv[:, l, co, 2:3], dstc, ALU.mult, ALU.add, sig=(Ux is Ug))
                    if Ux is Ug:
                        t_cg = t
                act.wait(t_cg)
                t_sg = ACT(sgl[par], cgp, AF.Silu, sig=True)
                dve.wait(t_sg)
                if cpi == 0:
                    dve.wait(tk_dn.get(T - 1))
                tk_a[gi] = TT(dve, aTv[:, cpi, :], sgl[par], cv, ALU.mult, sig=True)
            tk_st[T] = []
            for s4 in range(4):
                gd = gpc["d"]
                gpc["d"] += 1
                pD = psD[gd % 2]
                pe.wait(tk_a[T * NPAIR + NPAIR - 1], twd, tk_dn.get(("d", gd - 2)))
                for n2 in range(2):
                    for cc in range(NPAIR):
                        t = MM(pD[:, n2 * 512:(n2 + 1) * 512], aTv[:, cc, s4 * 128:(s4 + 1) * 128],
                               wdnv[:, cc, n2 * 512:(n2 + 1) * 512], cc == 0, cc == NPAIR - 1,
                               sig=(n2 == 1 and cc == NPAIR - 1))
                tk_dn[T] = t
                dve.wait(t)
                xs = x5[:, s4, :]
                TT(dve, xs[:, 0:512], xs[:, 0:512], pD[:, 0:512], ALU.add)
                t = TT(dve, xs[:, 512:1024], xs[:, 512:1024], pD[:, 512:1024], ALU.add, sig=True)
                tk_dn[("d", gd)] = t
                if last:
                    cc0 = c0 + 32 + s4 * 4
                    act.wait(t)
                    t = ACT(junk, xs, AF.Square, sig=True, accum_out=st[:, cc0:cc0 + 1])
                    t = rstd_chain(st[:, cc0:cc0 + 1], st[:, cc0 + 1:cc0 + 2], st[:, cc0 + 2:cc0 + 3],
                                   st[:, cc0 + 3:cc0 + 4], 1.0 / D, t)
                    dve.wait(t)
                    t = STT(dve, xs, xs, st[:, cc0 + 3:cc0 + 4], gfin_t[:], ALU.mult, ALU.mult, sig=True)
                pool.wait(t)
                r0 = T * 512 + s4 * 128
                tks = pool.dma(dst[r0:r0 + 128, :], xs, sxs[slx])
                tk_st[T].append(tks)
                pend.append(tks)

    setup()
    if stop != "setup":
        prepass()
    for s in range(nseq):
        for l in range(nl):
            if stop in ("setup", "prepass"):
                break
            phaseP(s, l)
            if stop == "P":
                break
            phaseATT(s, l)
            if stop == "ATT":
                break
            phaseO(s, l)
            if stop == "O":
                break
            phaseF(s, l, l == nl - 1)
    barrier()

    for so in prog.sems:
        so.h = es.enter_context(nc.semaphore())
    with nc.Block() as block:
        @block.tensor
        def _(e):
            pe.replay(e)

        @block.scalar
        def _(e):
            act.replay(e)

        @block.vector
        def _(e):
            dve.replay(e)

        @block.gpsimd
        def _(e):
            pool.replay(e)

        @block.sync
        def _(e):
            sp.replay(e)
    es.close()
    return nc


_CONST = {}


def host_consts():
    if not _CONST:
        cos, sin = make_rope()
        _CONST["cos"] = cos
        _CONST["sin"] = sin
        _CONST["masks"] = make_masks()
        _CONST["ident"] = np.eye(128, dtype=np.float32)
    return _CONST


def shared_inputs(attn_norm, w_in, q_norm, k_norm, rel_bias, w_out, ffn_norm, w_up, conv_w, conv_b, w_down,
                  final_norm):
    f = lambda a: np.ascontiguousarray(np.asarray(a, dtype=np.float32))
    c = host_consts()
    nl = NLAYER
    an_col = f(np.asarray(attn_norm).reshape(nl, 8, 128).transpose(0, 2, 1))
    fn_col = f(np.asarray(ffn_norm).reshape(nl, 8, 128).transpose(0, 2, 1))
    gq_rep = f(np.broadcast_to(np.asarray(q_norm).reshape(1, nl * 64), (128, nl * 64)))
    gk_rep = f(np.broadcast_to(np.asarray(k_norm).reshape(1, nl * 64), (128, nl * 64)))
    gfin_rep = f(np.broadcast_to(np.asarray(final_norm).reshape(1, D), (128, D)))
    relb_rep = f(np.broadcast_to(np.asarray(rel_bias).reshape(1, 256), (128, 256)))
    cw = np.asarray(conv_w).reshape(nl, 3, 44, 128)
    cb = np.asarray(conv_b).reshape(nl, 1, 44, 128)
    cp = np.concatenate([cw, cb], axis=1)
    convp = f(cp.transpose(3, 0, 2, 1).reshape(128, nl * 44 * 4))
    return {
        "w_in": f(w_in), "w_out": f(w_out), "w_up": f(w_up), "w_down": f(w_down),
        "an_col": an_col, "fn_col": fn_col, "gq_rep": gq_rep, "gk_rep": gk_rep, "gfin_rep": gfin_rep,
        "relb_rep": relb_rep, "convp": convp, "ident": c["ident"], "cos": c["cos"], "sin": c["sin"],
        "masks": c["masks"],
    }


def kernel(x_prompt, x_sample, attn_norm, w_in, q_norm, k_norm, rel_bias, w_out, ffn_norm, w_up, conv_w, conv_b,
           w_down, final_norm):
    xp = np.asarray(x_prompt, dtype=np.float32)
    xs = np.asarray(x_sample, dtype=np.float32)
    nb_p = xp.shape[0]
    x_all = np.concatenate([xp, xs], axis=0)
    shared = shared_inputs(attn_norm, w_in, q_norm, k_norm, rel_bias, w_out, ffn_norm, w_up, conv_w, conv_b, w_down,
                           final_norm)
    nc = build()
    in_maps = []
    for cix in range(NCORES):
        m = dict(shared)
        m["x"] = np.ascontiguousarray(x_all[cix * SEQ_PER_CORE:(cix + 1) * SEQ_PER_CORE])
        in_maps.append(m)
    res = run_bass_kernel_spmd(nc, in_maps, core_ids=list(range(NCORES)))
    y_all = np.concatenate([np.asarray(r["y"], dtype=np.float32) for r in res.results], axis=0)
    return (np.ascontiguousarray(y_all[:nb_p]), np.ascontiguousarray(y_all[nb_p:]))
```
